# Optimizing a Trainium2 kernel written in Bass

```python
import math
import jax, jax.numpy as jnp
from jax import lax
import numpy as np

D_MODEL = 2048
BATCH = 2
SEQ = 4096
DEPTH = 2

N_A_LAYERS = (DEPTH + 1) // 2
N_B_LAYERS = DEPTH // 2
SB_HEADS = 16
SB_HEAD_DIM = D_MODEL // SB_HEADS
DIFF_HEADS = 8
DIFF_HEAD_DIM = D_MODEL // DIFF_HEADS // 2
DIFF_V_DIM = 2 * DIFF_HEAD_DIM
D_FF = -(-8 * D_MODEL // (3 * 256)) * 256
Q_BLOCK = 128
ROPE_THETA = 10000.0
EPS = 1e-6
LAMBDA_PARAM_STD = 0.1

kernel_name = "yoco_stickbreak_diffattn_hybrid"


def rmsnorm(x, g):
    x32 = x.astype(jnp.float32)
    y = x32 * lax.rsqrt(jnp.mean(x32 * x32, axis=-1, keepdims=True) + EPS)
    return (y * g.astype(jnp.float32)).astype(x.dtype)


def rope(x):
    S, d = x.shape[1], x.shape[-1]
    inv_freq = ROPE_THETA ** (-jnp.arange(0, d, 2, dtype=jnp.float32) / d)
    ang = jnp.arange(S, dtype=jnp.float32)[:, None] * inv_freq[None, :]
    ang = jnp.concatenate([ang, ang], axis=-1)
    bshape = (S,) + (1,) * (x.ndim - 3) + (d,)
    cos = jnp.cos(ang).reshape(bshape)
    sin = jnp.sin(ang).reshape(bshape)
    x32 = x.astype(jnp.float32)
    x1, x2 = x32[..., : d // 2], x32[..., d // 2:]
    rot = jnp.concatenate([-x2, x1], axis=-1)
    return (x32 * cos + rot * sin).astype(x.dtype)


def swiglu(h, w_gate_up, w_down):
    gu = h @ w_gate_up
    g, u = gu[..., :D_FF], gu[..., D_FF:]
    return (jax.nn.silu(g) * u) @ w_down


def stick_breaking_attention(q, k, v):
    B, S, H, d = q.shape
    nb = S // Q_BLOCK
    qh = q.transpose(0, 2, 1, 3).astype(jnp.float32) * (d ** -0.5)
    kh = k.transpose(0, 2, 1, 3).astype(jnp.float32)
    vh = v.transpose(0, 2, 1, 3).astype(jnp.float32)
    qb = qh.reshape(B, H, nb, Q_BLOCK, d).transpose(2, 0, 1, 3, 4)
    kpos = jnp.arange(S)

    def block(args):
        q_blk, start = args
        qpos = start + jnp.arange(Q_BLOCK)
        z = jnp.einsum('bhqd,bhkd->bhqk', q_blk, kh)
        mask = kpos[None, :] < qpos[:, None]
        log_beta = jax.nn.log_sigmoid(z)
        log_1m = jnp.where(mask, jax.nn.log_sigmoid(-z), 0.0)
        later = lax.cumsum(log_1m, axis=3, reverse=True) - log_1m
        a = jnp.where(mask, jnp.exp(log_beta + later), 0.0)
        return jnp.einsum('bhqk,bhkd->bhqd', a, vh)

    starts = jnp.arange(nb) * Q_BLOCK
    out = lax.map(block, (qb, starts))
    return out.transpose(1, 0, 3, 2, 4).reshape(B, S, H * d).astype(v.dtype)


def differential_attention(q, k, v, lam):
    B, S, H, _, d = q.shape
    dv = v.shape[-1]
    nb = S // Q_BLOCK
    qh = q.transpose(0, 2, 3, 1, 4).astype(jnp.float32) * (d ** -0.5)
    kh = k.transpose(0, 2, 3, 1, 4).astype(jnp.float32)
    vh = v.transpose(0, 2, 1, 3).astype(jnp.float32)
    qb = qh.reshape(B, H, 2, nb, Q_BLOCK, d).transpose(3, 0, 1, 2, 4, 5)
    kpos = jnp.arange(S)

    def block(args):
        q_blk, start = args
        qpos = start + jnp.arange(Q_BLOCK)
        s = jnp.einsum('bhcqd,bhckd->bhcqk', q_blk, kh)
        mask = kpos[None, :] <= qpos[:, None]
        p = jax.nn.softmax(jnp.where(mask, s, -jnp.inf), axis=-1)
        w = p[:, :, 0] - lam * p[:, :, 1]
        return jnp.einsum('bhqk,bhkv->bhqv', w, vh)

    starts = jnp.arange(nb) * Q_BLOCK
    out = lax.map(block, (qb, starts))
    return out.transpose(1, 0, 3, 2, 4).reshape(B, S, H, dv).astype(v.dtype)


def setup_inputs(seed: int = 0) -> dict:
    key = jax.random.key(seed)
    ks = jax.random.split(key, 24)
    D, F = D_MODEL, D_FF

    def dense(k, shape):
        return jax.random.normal(k, shape, jnp.float32) * (shape[-2] ** -0.5)

    def gain(k, shape):
        return 1.0 + 0.02 * jax.random.normal(k, shape, jnp.float32)

    def small(k, shape, std):
        return std * jax.random.normal(k, shape, jnp.float32)

    return {
        "x": jax.random.normal(ks[0], (BATCH, SEQ, D), jnp.float32),
        "a_attn_norm": gain(ks[1], (N_A_LAYERS, D)),
        "a_w_qkv": dense(ks[2], (N_A_LAYERS, D, 3 * SB_HEADS * SB_HEAD_DIM)),
        "a_w_o": dense(ks[3], (N_A_LAYERS, SB_HEADS * SB_HEAD_DIM, D)),
        "a_ffn_norm": gain(ks[4], (N_A_LAYERS, D)),
        "a_w_gate_up": dense(ks[5], (N_A_LAYERS, D, 2 * F)),
        "a_w_down": dense(ks[6], (N_A_LAYERS, F, D)),
        "kv_norm": gain(ks[7], (D,)),
        "w_kv": dense(ks[8], (D, DIFF_HEADS * 2 * DIFF_HEAD_DIM + DIFF_HEADS * DIFF_V_DIM)),
        "k_norm": gain(ks[9], (DIFF_HEAD_DIM,)),
        "b_attn_norm": gain(ks[10], (N_B_LAYERS, D)),
        "b_w_q": dense(ks[11], (N_B_LAYERS, D, DIFF_HEADS * 2 * DIFF_HEAD_DIM)),
        "b_q_norm": gain(ks[12], (N_B_LAYERS, DIFF_HEAD_DIM)),
        "b_lambda_q1": small(ks[13], (N_B_LAYERS, DIFF_HEAD_DIM), LAMBDA_PARAM_STD),
        "b_lambda_k1": small(ks[14], (N_B_LAYERS, DIFF_HEAD_DIM), LAMBDA_PARAM_STD),
        "b_lambda_q2": small(ks[15], (N_B_LAYERS, DIFF_HEAD_DIM), LAMBDA_PARAM_STD),
        "b_lambda_k2": small(ks[16], (N_B_LAYERS, DIFF_HEAD_DIM), LAMBDA_PARAM_STD),
        "b_subln": gain(ks[17], (N_B_LAYERS, DIFF_V_DIM)),
        "b_w_o": dense(ks[18], (N_B_LAYERS, DIFF_HEADS * DIFF_V_DIM, D)),
        "b_ffn_norm": gain(ks[19], (N_B_LAYERS, D)),
        "b_w_gate_up": dense(ks[20], (N_B_LAYERS, D, 2 * F)),
        "b_w_down": dense(ks[21], (N_B_LAYERS, F, D)),
    }


def reference(x, a_attn_norm, a_w_qkv, a_w_o, a_ffn_norm, a_w_gate_up, a_w_down,
              kv_norm, w_kv, k_norm,
              b_attn_norm, b_w_q, b_q_norm, b_lambda_q1, b_lambda_k1, b_lambda_q2, b_lambda_k2,
              b_subln, b_w_o, b_ffn_norm, b_w_gate_up, b_w_down):
    B, S, D = x.shape
    HD_SB = SB_HEADS * SB_HEAD_DIM
    HK = DIFF_HEADS * 2 * DIFF_HEAD_DIM

    for layer in range(DEPTH):
        if layer < N_A_LAYERS:
            i = layer
            h = rmsnorm(x, a_attn_norm[i])
            qkv = h @ a_w_qkv[i]
            q = qkv[..., :HD_SB].reshape(B, S, SB_HEADS, SB_HEAD_DIM)
            k = qkv[..., HD_SB:2 * HD_SB].reshape(B, S, SB_HEADS, SB_HEAD_DIM)
            v = qkv[..., 2 * HD_SB:].reshape(B, S, SB_HEADS, SB_HEAD_DIM)
            x = x + stick_breaking_attention(q, k, v) @ a_w_o[i]
            x = x + swiglu(rmsnorm(x, a_ffn_norm[i]), a_w_gate_up[i], a_w_down[i])
            if layer == N_A_LAYERS - 1:
                kv = rmsnorm(x, kv_norm) @ w_kv
                k_sh = kv[..., :HK].reshape(B, S, DIFF_HEADS, 2, DIFF_HEAD_DIM)
                k_sh = rope(rmsnorm(k_sh, k_norm))
                v_sh = kv[..., HK:].reshape(B, S, DIFF_HEADS, DIFF_V_DIM)
        else:
            i = layer - N_A_LAYERS
            lambda_init = 0.8 - 0.6 * math.exp(-0.3 * layer)
            h = rmsnorm(x, b_attn_norm[i])
            q = (h @ b_w_q[i]).reshape(B, S, DIFF_HEADS, 2, DIFF_HEAD_DIM)
            q = rope(rmsnorm(q, b_q_norm[i]))
            lam = (jnp.exp(jnp.sum(b_lambda_q1[i].astype(jnp.float32) * b_lambda_k1[i].astype(jnp.float32)))
                   - jnp.exp(jnp.sum(b_lambda_q2[i].astype(jnp.float32) * b_lambda_k2[i].astype(jnp.float32)))
                   + lambda_init)
            o = differential_attention(q, k_sh, v_sh, lam)
            o = rmsnorm(o, b_subln[i]) * (1.0 - lambda_init)
            x = x + o.reshape(B, S, DIFF_HEADS * DIFF_V_DIM) @ b_w_o[i]
            x = x + swiglu(rmsnorm(x, b_ffn_norm[i]), b_w_gate_up[i], b_w_down[i])
    return x
```

```python
import math
from contextlib import ExitStack

import numpy as np
import ml_dtypes

import concourse.bass as bass
import concourse.mybir as mybir
from concourse.bass_utils import run_bass_kernel_spmd

F32 = mybir.dt.float32
BF16 = mybir.dt.bfloat16
AF = mybir.ActivationFunctionType
ALU = mybir.AluOpType
BF = ml_dtypes.bfloat16

D = 2048
KC = 16
T = 1024
S = 4096
NTG = 2
TG = 512
FF = 5632
FC = 44
FQS = [8, 8, 7, 7, 7, 7]
NQ = len(FQS)
EPS = 1e-6
NEG = -30000.0
ROPE_THETA = 10000.0
LAMBDA_INIT = 0.8 - 0.6 * math.exp(-0.3 * 1)
SCALE = 128.0 ** -0.5
NDS = 12


class Buf:
    __slots__ = ("name", "w", "r")

    def __init__(self, name):
        self.name = name
        self.w = None
        self.r = []


class Prog:
    ENG = ("pe", "act", "dve", "pool", "sp")

    def __init__(self, nc, stack):
        self.nc = nc
        self.q = {e: [] for e in self.ENG}
        self.sem = {e: stack.enter_context(nc.semaphore("pg_" + e)) for e in self.ENG}
        self.cnt = dict.fromkeys(self.ENG, 0)
        self.waited = {e: {} for e in self.ENG}
        self.pend_r = {e: [] for e in self.ENG}
        self.pend_w = {e: [] for e in self.ENG}
        self.dsem = {e: [stack.enter_context(nc.semaphore("d_%s%d" % (e, i))) for i in range(NDS)]
                     for e in ("sp", "pool")}
        self.dcnt = {e: [0] * NDS for e in ("sp", "pool")}
        self.dnext = {"sp": 0, "pool": 0}
        self.out_toks = []

    def _wait(self, eng, tok):
        sem, val, src = tok
        if src == "pe" and eng == "pe":
            return
        key = id(sem)
        if self.waited[eng].get(key, 0) >= val:
            return
        self.waited[eng][key] = val
        self.q[eng].append(lambda e, sem=sem, val=val: e.wait_ge(sem, val))

    def _deps(self, eng, reads, writes):
        for b in reads:
            if b.w is not None:
                self._wait(eng, b.w)
        for b in writes:
            if b.w is not None:
                self._wait(eng, b.w)
            for t in b.r:
                self._wait(eng, t)

    def op(self, eng, call, reads=(), writes=(), mark=True):
        name, kw = call
        pos = kw.pop("_pos", ())
        fn = (lambda e, name=name, kw=kw, pos=pos: getattr(e, name)(*pos, **kw))
        self._deps(eng, reads, writes)
        if not mark:
            self.q[eng].append(lambda e, fn=fn: fn(e))
            self.pend_r[eng].extend(reads)
            self.pend_w[eng].extend(writes)
            return None
        self.cnt[eng] += 1
        sem = self.sem[eng]
        tok = (sem, self.cnt[eng], eng)
        self.q[eng].append(lambda e, fn=fn, sem=sem: fn(e).then_inc(sem, 1))
        for b in self.pend_r[eng]:
            b.r.append(tok)
        for b in self.pend_w[eng]:
            b.w = tok
            b.r = []
        self.pend_r[eng] = []
        self.pend_w[eng] = []
        for b in reads:
            b.r.append(tok)
        for b in writes:
            b.w = tok
            b.r = []
        return tok

    def dma(self, eng, out, in_, reads=(), writes=(), is_out=False):
        i = self.dnext[eng]
        self.dnext[eng] = (i + 1) % NDS
        sem = self.dsem[eng][i]
        prev = self.dcnt[eng][i]
        if prev > 0:
            self._wait(eng, (sem, prev, "dma"))
        self._deps(eng, reads, writes)
        self.dcnt[eng][i] = prev + 16
        tok = (sem, prev + 16, "dma")
        self.q[eng].append(lambda e, out=out, in_=in_, sem=sem: e.dma_start(out=out, in_=in_).then_inc(sem, 16))
        for b in reads:
            b.r.append(tok)
        for b in writes:
            b.w = tok
            b.r = []
        if is_out:
            self.out_toks.append(tok)
        return tok

    def emit(self):
        for tok in self.out_toks:
            self._wait("sp", tok)
        for e in ("pe", "act", "dve", "pool"):
            if self.cnt[e] > 0:
                self._wait("sp", (self.sem[e], self.cnt[e], e))
        for e in ("sp", "pool"):
            for i in range(NDS):
                if self.dcnt[e][i] > 0:
                    self._wait("sp", (self.dsem[e][i], self.dcnt[e][i], "dma"))
        q = self.q
        with self.nc.Block() as block:
            @block.tensor
            def _(e):
                for f in q["pe"]:
                    f(e)

            @block.scalar
            def _(e):
                for f in q["act"]:
                    f(e)

            @block.vector
            def _(e):
                for f in q["dve"]:
                    f(e)

            @block.gpsimd
            def _(e):
                for f in q["pool"]:
                    f(e)

            @block.sync
            def _(e):
                for f in q["sp"]:
                    f(e)


class Ring:
    def __init__(self, items):
        self.items = items
        self.i = 0

    def next(self):
        it = self.items[self.i]
        self.i = (self.i + 1) % len(self.items)
        return it


class K:
    def __init__(self, nc, stack):
        self.nc = nc
        self.stack = stack
        self.p = Prog(nc, stack)
        self.nalloc = 0
        self.banks = Ring([(self.psum("bank%d" % i), Buf("bank%d" % i)) for i in range(8)])

    def sb(self, shape, dtype, name=None):
        self.nalloc += 1
        t = self.stack.enter_context(self.nc.sbuf_tensor("sb_" + (name or ("t%d" % self.nalloc)), list(shape), dtype))
        return t

    def psum(self, name):
        return self.stack.enter_context(self.nc.psum_tensor("ps_" + name, [128, 512], F32))

    def din(self, name, shape, dtype=F32):
        return self.nc.dram_tensor(name, list(shape), dtype, kind="ExternalInput").ap()

    def dout(self, name, shape, dtype=F32):
        return self.nc.dram_tensor(name, list(shape), dtype, kind="ExternalOutput").ap()

    def consts(self, cm_ap):
        p = self.p
        self.cm = self.sb([128, 6 * 128], BF16, "cm")
        self.cm_b = Buf("cm")
        p.dma("sp", self.cm[:], cm_ap, writes=[self.cm_b])
        self.ident = self.cm[:, 0:128]
        self.negtri = self.cm[:, 128:256]
        self.sel33 = self.cm[0:33, 256:384]
        self.maskA = self.cm[:, 384:512]
        self.maskB = self.cm[:, 512:640]
        self.ones = self.cm[:, 640:768]
        self.cf = self.sb([128, 4], F32, "cf")
        self.cf_b = Buf("cf")
        p.op("dve", ("memset", dict(_pos=(self.cf[:, 0:1], EPS))), writes=[self.cf_b])
        p.op("dve", ("memset", dict(_pos=(self.cf[:, 1:2], 1.0))), writes=[self.cf_b])
        p.op("dve", ("memset", dict(_pos=(self.cf[:, 2:3], 0.0))), writes=[self.cf_b])
        self.eps = self.cf[:, 0:1]
        self.one = self.cf[:, 1:2]
        self.zero = self.cf[:, 2:3]
        self.negones = self.sb([128, 128], BF16, "negones")
        self.zeros = self.sb([128, 512], BF16, "zeros")
        self.cz_b = Buf("cz")
        p.op("dve", ("memset", dict(_pos=(self.negones[:], -1.0))), writes=[self.cz_b])
        p.op("dve", ("memset", dict(_pos=(self.zeros[:], 0.0))), writes=[self.cz_b])
        self.cbufs = [self.cm_b, self.cf_b, self.cz_b]


def mm(k, out, lhsT, rhs, start, stop, reads, writes, mark):
    return k.p.op("pe", ("matmul", dict(_pos=(out,), lhsT=lhsT, rhs=rhs, start=start, stop=stop)),
                  reads=reads, writes=writes, mark=mark)


class TokState:
    def __init__(self, k, need_act=True):
        self.k = k
        self.x = k.sb([128, KC, T], F32, "xT")
        self.xb = [[Buf("x%d_%d" % (m, tg)) for tg in range(NTG)] for m in range(KC)]
        self.h = k.sb([128, KC, T], BF16, "hT")
        self.hb = [Buf("h%d" % tg) for tg in range(NTG)]
        self.qo = k.sb([128, KC, T], BF16, "qo")
        self.qob = [[Buf("qo%d_%d" % (m, tg)) for tg in range(NTG)] for m in range(KC)]
        self.wslots = Ring([(k.sb([128, 4096], BF16, "w%d" % i), Buf("w%d" % i)) for i in range(4)])
        self.sq = Ring([(k.sb([128, TG], BF16, "sq%d" % i), Buf("sq%d" % i)) for i in range(2)])
        self.rstd = Ring([(k.sb([128, TG], F32, "rstd%d" % i), Buf("rstd%d" % i)) for i in range(2)])
        self.tmpf = Ring([(k.sb([128, TG], F32, "tmpf%d" % i), Buf("tmpf%d" % i)) for i in range(4)])
        self.stg = Ring([(k.sb([128, 2048], BF16, "stg%d" % i), Buf("stg%d" % i)) for i in range(2)])
        self.gains = k.sb([128, 8 * KC], F32, "gains")
        self.gains_b = Buf("gains")
        self.evac_i = 0

    def wload(self, dram_ap, n):
        k = self.k
        slot, b = self.wslots.next()
        k.p.dma("pool", slot[:, 0:n], dram_ap, writes=[b])
        return slot, b


def load_x(k, st, x_dram):
    for kc in range(KC):
        k.p.dma("sp", st.x[:, kc, :], x_dram[kc * 128:(kc + 1) * 128, :], writes=st.xb[kc])


def store_x(k, st, x_dram):
    for kc in range(KC):
        k.p.dma("sp", x_dram[kc * 128:(kc + 1) * 128, :], st.x[:, kc, :], reads=st.xb[kc], is_out=True)


def rmsnorm(k, st, gcol):
    p = k.p
    for tg in range(NTG):
        ts = slice(tg * TG, (tg + 1) * TG)
        bank, bb = k.banks.next()
        for kc in range(KC):
            sq, sqb = st.sq.next()
            p.op("act", ("activation", dict(out=sq[:], in_=st.x[:, kc, ts], func=AF.Square)),
                 reads=[st.xb[kc][tg]], writes=[sqb])
            mm(k, bank[:], k.ones, sq[:], kc == 0, kc == KC - 1, [sqb] + k.cbufs, [bb], True)
        rs, rsb = st.rstd.next()
        p.op("act", ("activation", dict(out=rs[:], in_=bank[:], func=AF.Sqrt,
                                                             bias=k.eps, scale=1.0 / D)),
             reads=[bb] + k.cbufs, writes=[rsb])
        p.op("dve", ("reciprocal", dict(out=rs[:], in_=rs[:])), reads=[rsb], writes=[rsb])
        for kc in range(KC):
            p.op("dve", ("scalar_tensor_tensor", dict(
                out=st.h[:, kc, ts], in0=st.x[:, kc, ts], scalar=st.gains[:, gcol + kc:gcol + kc + 1], in1=rs[:],
                op0=ALU.mult, op1=ALU.mult)),
                 reads=[st.xb[kc][tg], rsb, st.gains_b], writes=[st.hb[tg]])


def linear_fm(k, st, w_dram, nload, cpl, nkc, rhs_fn, rhs_bufs_fn, evac, kpad=None):
    kp = kpad or nkc
    for l in range(nload):
        slot, wb = st.wload(w_dram[l], cpl * kp * 128)
        for ci in range(cpl):
            m = l * cpl + ci
            for tg in range(NTG):
                bank, bb = k.banks.next()
                for kc in range(nkc):
                    o = (ci * kp + kc) * 128
                    mm(k, bank[:], slot[:, o:o + 128], rhs_fn(kc, tg), kc == 0, kc == nkc - 1,
                       [wb] + rhs_bufs_fn(kc, tg), [bb], kc == nkc - 1)
                evac(m, tg, bank, bb)


def evac_scaled(k, st, out_ap, bank, bb, scale, wbufs):
    st.evac_i += 1
    if st.evac_i % 2 == 0:
        k.p.op("act", ("activation", dict(out=out_ap, in_=bank[:], func=AF.Copy, scale=scale)),
               reads=[bb], writes=wbufs)
    else:
        k.p.op("dve", ("tensor_scalar", dict(out=out_ap, in0=bank[:], scalar1=scale, scalar2=None, op0=ALU.mult)),
               reads=[bb], writes=wbufs)


def evac_resid(k, st, m, tg, bank, bb):
    ts = slice(tg * TG, (tg + 1) * TG)
    k.p.op("dve", ("tensor_tensor", dict(out=st.x[:, m, ts], in0=st.x[:, m, ts], in1=bank[:], op=ALU.add)),
           reads=[bb], writes=[st.xb[m][tg]])


def linear_tm_v(k, st, w_dram, nload, ncol, out_fn):
    p = k.p
    for l in range(nload):
        slot, wb = st.wload(w_dram[l], KC * ncol)
        vt, vb = st.stg.next()
        vacc = vt[:, :].rearrange("p (t c) -> p t c", c=ncol)
        for tb in range(8):
            tg = tb // 4
            bank, bb = k.banks.next()
            for kc in range(KC):
                mm(k, bank[:, 0:ncol], st.h[:, kc, tb * 128:(tb + 1) * 128], slot[:, kc * ncol:(kc + 1) * ncol],
                   kc == 0, kc == KC - 1, [wb, st.hb[tg]], [bb], kc == KC - 1)
            st.evac_i += 1
            if st.evac_i % 2 == 0:
                p.op("act", ("activation", dict(
                    out=vacc[:, tb, :], in_=bank[:, 0:ncol], func=AF.Copy)), reads=[bb], writes=[vb])
            else:
                p.op("dve", ("tensor_copy", dict(
                    out=vacc[:, tb, :], in_=bank[:, 0:ncol])), reads=[bb], writes=[vb])
        out_fn(l, vacc, vb)


def ffn(k, st, wgu, wd, gcol):
    p = k.p
    rmsnorm(k, st, gcol)
    acts = [(st.qo[:, 0:8, :], Buf("act0")), (st.qo[:, 8:16, :], Buf("act1"))]
    for a in range(2):
        for m in range(8 * a, 8 * a + 8):
            for tg in range(NTG):
                b = st.qob[m][tg]
                if b.w is not None:
                    acts[a][1].r.append(b.w)
                acts[a][1].r.extend(b.r)
    f0 = 0
    for q in range(NQ):
        fq = FQS[q]
        act, ab = acts[q % 2]
        for f in range(fq):
            slot, wb = st.wload(wgu[f0 + f], 2 * KC * 128)
            for tg in range(NTG):
                ts = slice(tg * TG, (tg + 1) * TG)
                bg, bgb = k.banks.next()
                bu, bub = k.banks.next()
                for kc in range(KC):
                    mm(k, bg[:], slot[:, kc * 128:(kc + 1) * 128], st.h[:, kc, ts], kc == 0, kc == KC - 1,
                       [wb, st.hb[tg]], [bgb], kc == KC - 1)
                for kc in range(KC):
                    o = (KC + kc) * 128
                    mm(k, bu[:], slot[:, o:o + 128], st.h[:, kc, ts], kc == 0, kc == KC - 1,
                       [wb, st.hb[tg]], [bub], kc == KC - 1)
                sg, sgb = st.tmpf.next()
                p.op("act", ("activation", dict(out=sg[:], in_=bg[:], func=AF.Silu)),
                     reads=[bgb], writes=[sgb])
                p.op("dve", ("tensor_tensor", dict(
                    out=act[:, f, ts], in0=sg[:], in1=bu[:], op=ALU.mult)),
                     reads=[sgb, bub], writes=[ab])
        f0 += fq
        linear_fm(k, st, wd[q], 8, 2, fq,
                  lambda kc, tg, act=act: act[:, kc, tg * TG:(tg + 1) * TG],
                  lambda kc, tg, ab=ab: [ab],
                  lambda m, tg, bank, bb: evac_resid(k, st, m, tg, bank, bb), kpad=8)


def qknorm_rope_chunk(k, st, rp, bank, bb, tg, out_ap, out_bufs, extra_scale):
    p = k.p
    ts = slice(tg * TG, (tg + 1) * TG)
    raw, rawb = st.tmpf.next()
    p.op("act", ("activation", dict(out=raw[:], in_=bank[:], func=AF.Copy)), reads=[bb], writes=[rawb])
    sw, swb = rp.sw.next()
    p.dma("sp", sw[0:64, :], raw[64:128, :], reads=[rawb], writes=[swb])
    p.dma("sp", sw[64:128, :], raw[0:64, :], reads=[rawb], writes=[swb])
    sq, sqb = st.sq.next()
    p.op("act", ("activation", dict(out=sq[:], in_=raw[:], func=AF.Square)), reads=[rawb], writes=[sqb])
    b2, b2b = k.banks.next()
    mm(k, b2[:], k.ones, sq[:], True, True, [sqb] + k.cbufs, [b2b], True)
    rs, rsb = st.rstd.next()
    p.op("act", ("activation", dict(out=rs[:], in_=b2[:], func=AF.Sqrt, bias=k.eps, scale=1.0 / 128)),
         reads=[b2b] + k.cbufs, writes=[rsb])
    p.op("dve", ("reciprocal", dict(out=rs[:], in_=rs[:])), reads=[rsb], writes=[rsb])
    t1, t1b = st.tmpf.next()
    p.op("dve", ("tensor_tensor", dict(out=t1[:], in0=raw[:], in1=rp.cg[:, ts], op=ALU.mult)),
         reads=[rawb, rp.tab_b], writes=[t1b])
    p.op("pool", ("tensor_tensor", dict(out=sw[:], in0=sw[:], in1=rp.sg[:, ts], op=ALU.mult)),
         reads=[swb, rp.tab_b], writes=[swb])
    p.op("dve", ("tensor_tensor", dict(out=t1[:], in0=t1[:], in1=sw[:], op=ALU.add)),
         reads=[t1b, swb], writes=[t1b])
    p.op("dve", ("scalar_tensor_tensor", dict(out=out_ap, in0=t1[:], scalar=float(extra_scale), in1=rs[:],
                                                 op0=ALU.mult, op1=ALU.mult)),
         reads=[t1b, rsb], writes=out_bufs)


class RopeState:
    def __init__(self, k):
        self.sw = Ring([(k.sb([128, TG], F32, "sw%d" % i), Buf("sw%d" % i)) for i in range(2)])
        self.cg = k.sb([128, T], F32, "cg")
        self.sg = k.sb([128, T], F32, "sg")
        self.tab_b = Buf("ropetab")
        self.gn = k.sb([128, 4], F32, "gn")
        self.gn_b = Buf("gn")


def rope_tables(k, rp, cos_d, sin_d, col):
    p = k.p
    p.dma("sp", rp.cg[:], cos_d, writes=[rp.tab_b])
    p.dma("sp", rp.sg[:], sin_d, writes=[rp.tab_b])
    p.op("dve", ("tensor_scalar", dict(out=rp.cg[:], in0=rp.cg[:], scalar1=rp.gn[:, col:col + 1], scalar2=None,
                                          op0=ALU.mult)), reads=[rp.gn_b, rp.tab_b], writes=[rp.tab_b])
    p.op("dve", ("tensor_scalar", dict(out=rp.sg[:], in0=rp.sg[:], scalar1=rp.gn[:, col + 1:col + 2],
                                          scalar2=None, op0=ALU.mult)), reads=[rp.gn_b, rp.tab_b], writes=[rp.tab_b])


def build_p1():
    nc = bass.Bass("TRN2", target_bir_lowering=False)
    with ExitStack() as stack:
        k = K(nc, stack)
        x_d = k.din("xT", [D, T])
        cm_d = k.din("cm", [128, 768], BF16)
        g_d = k.din("gains", [128, 8 * KC])
        wqk_d = k.din("wqk", [16, 128, 4096])
        wv_d = k.din("wv", [8, 128, 4096])
        q_o = k.dout("qT", [D, T], BF16)
        k_o = k.dout("kT", [D, T], BF16)
        v_o = k.dout("v", [16, 128, 8, 128], BF16)
        k.consts(cm_d)
        st = TokState(k)
        kst = st.stg
        k.p.dma("sp", st.gains[:], g_d, writes=[st.gains_b])
        load_x(k, st, x_d)
        rmsnorm(k, st, 0)
        cur = {}

        def evac_qk(m, tg, bank, bb):
            ts = slice(tg * TG, (tg + 1) * TG)
            if tg == 0:
                cur["s"] = kst.next()
            s, sb_ = cur["s"]
            evac_scaled(k, st, s[:, ts], bank, bb, SCALE if m < 16 else 1.0, [sb_])
            if tg == 1:
                dst = q_o if m < 16 else k_o
                mm_ = m % 16
                k.p.dma("sp", dst[mm_ * 128:(mm_ + 1) * 128, :], s[:, 0:T], reads=[sb_], is_out=True)

        linear_fm(k, st, wqk_d, 16, 2, KC, lambda kc, tg: st.h[:, kc, tg * TG:(tg + 1) * TG],
                  lambda kc, tg: [st.hb[tg]], evac_qk)

        def out_v(l, vacc, vb):
            for hh in range(2):
                k.p.dma("sp", v_o[2 * l + hh], vacc[:, :, hh * 128:(hh + 1) * 128], reads=[vb], is_out=True)

        linear_tm_v(k, st, wv_d, 8, 256, out_v)
        k.p.emit()
    return nc


def attn_a_units():
    units = []
    for hh in range(4):
        for g in range(8):
            jmax = 4 * g + 3
            for j in range(jmax, -1, -1):
                c0 = max(0, (j - 4 * g) * 128)
                units.append(dict(hh=hh, g=g, j=j, c0=c0, first=(j == jmax), last=(j == 0), diag=(j >= 4 * g)))
    return units


def build_p2():
    nc = bass.Bass("TRN2", target_bir_lowering=False)
    with ExitStack() as stack:
        k = K(nc, stack)
        p = k.p
        cm_d = k.din("cm", [128, 768], BF16)
        q_d = k.din("qh", [4, 128, S], BF16)
        k_d = k.din("kh", [4, 128, S], BF16)
        v_d = k.din("vh", [4, 128, S], BF16)
        o_d = k.dout("oT", [4, 128, S], BF16)
        k.consts(cm_d)
        q_sb = k.sb([128, 4, S], BF16, "q_sb")
        k_sb = k.sb([128, 4, S], BF16, "k_sb")
        v_sb = k.sb([128, 4, S], BF16, "v_sb")
        qb = [[Buf("q%d_%d" % (h, g)) for g in range(8)] for h in range(4)]
        kb = [Buf("k%d" % h) for h in range(4)]
        vb = [Buf("v%d" % h) for h in range(4)]
        for h in range(4):
            p.dma("sp", q_sb[:, h, :], q_d[h], writes=qb[h])
            p.dma("sp", k_sb[:, h, :], k_d[h], writes=[kb[h]])
            p.dma("sp", v_sb[:, h, :], v_d[h], writes=[vb[h]])
        attn_a(k, q_sb, k_sb, v_sb, qb, kb, vb)
        for h in range(4):
            p.dma("sp", o_d[h], q_sb[:, h, :], reads=qb[h], is_out=True)
        p.emit()
    return nc


def attn_a(k, q_sb, k_sb, v_sb, qb, kb, vb):
    p = k.p
    allb = k.banks.items
    P1 = Ring(allb[0:2])
    P2 = Ring(allb[2:4])
    PO = Ring(allb[4:6])
    PC = Ring(allb[6:8])
    eb = Ring([(k.sb([128, TG], F32, "eb%d" % i), Buf("eb%d" % i)) for i in range(2)])
    spb = Ring([(k.sb([128, TG], BF16, "sp%d" % i), Buf("sp%d" % i)) for i in range(3)])
    ab = Ring([(k.sb([128, TG], BF16, "ab%d" % i), Buf("ab%d" % i)) for i in range(2)])
    cb = Ring([(k.sb([33, TG], BF16, "cb%d" % i), Buf("cb%d" % i)) for i in range(2)])
    cfr = Ring([(k.sb([33, TG], F32, "cfr%d" % i), Buf("cfr%d" % i)) for i in range(2)])
    import os
    units = attn_a_units()[:int(os.environ.get('ULIM', '100000'))]
    n = len(units)
    C = k.cbufs

    def s1(u):
        hh, g, j, c0 = u["hh"], u["g"], u["j"], u["c0"]
        u["P1"] = P1.next()
        bank, bb = u["P1"]
        kT = k_sb[:, hh, j * 128:(j + 1) * 128]
        qc = q_sb[:, hh, g * TG + c0:(g + 1) * TG]
        mm(k, bank[:, c0:TG], kT, qc, True, not u["diag"], [kb[hh], qb[hh][g]], [bb], not u["diag"])
        if u["diag"]:
            mm(k, bank[:, c0:c0 + 128], k.ident, k.maskA, False, True, C, [bb], True)

    def actA(u):
        c0 = u["c0"]
        bank, bb = u["P1"]
        e_, ebb = eb.next()
        p.op("act", ("activation", dict(out=e_[:, c0:TG], in_=bank[:, c0:TG], func=AF.Exp)),
             reads=[bb], writes=[ebb])
        u["sp"] = spb.next()
        sp, spbb = u["sp"]
        p.op("act", ("activation", dict(out=sp[:, c0:TG], in_=e_[:, c0:TG], func=AF.Ln, bias=k.one, scale=1.0)),
             reads=[ebb] + C, writes=[spbb])

    def s2(u, nxt):
        hh, g, j, c0 = u["hh"], u["g"], u["j"], u["c0"]
        sp, spbb = u["sp"]
        if u["first"]:
            u["PO"] = PO.next()
            mm(k, u["PO"][0][:], k.ident, k.zeros[:], True, False, C, [u["PO"][1]], True)
            u["cf"] = cfr.next()
            p.op("dve", ("memset", dict(_pos=(u["cf"][0][0:33, :], 0.0))), writes=[u["cf"][1]])
            u["cb"] = None
        pc, pcb = PC.next()
        mm(k, pc[:, c0:TG], k.negones[:], sp[:, c0:TG], True, True, [spbb] + C, [pcb], True)
        if not u["last"]:
            nc0 = nxt["c0"]
            cft, cfb = u["cf"]
            p.op("dve", ("tensor_tensor", dict(out=cft[0:33, c0:TG], in0=cft[0:33, c0:TG], in1=pc[0:33, c0:TG],
                                               op=ALU.add)), reads=[pcb, cfb], writes=[cfb])
            cbt, cbb = cb.next()
            p.op("dve", ("tensor_copy", dict(out=cbt[0:33, nc0:TG], in_=cft[0:33, nc0:TG])),
                 reads=[cfb], writes=[cbb])
            p.op("dve", ("tensor_tensor", dict(out=cbt[32:33, nc0:TG], in0=cft[32:33, nc0:TG],
                                               in1=cbt[32:33, nc0:TG], op=ALU.subtract)),
                 reads=[cfb, cbb], writes=[cbb])
            nxt["cb"] = (cbt, cbb)
            nxt["PO"] = u["PO"]
            nxt["cf"] = u["cf"]
        u["P2"] = P2.next()
        bank, bb = u["P2"]
        kT = k_sb[:, hh, j * 128:(j + 1) * 128]
        qc = q_sb[:, hh, g * TG + c0:(g + 1) * TG]
        mm(k, bank[:, c0:TG], kT, qc, True, False, [kb[hh], qb[hh][g]], [bb], False)
        has_c = u["cb"] is not None
        lastmm = not (has_c or u["diag"])
        mm(k, bank[:, c0:TG], k.negtri, sp[:, c0:TG], False, lastmm, [spbb] + C, [bb], lastmm)
        if has_c:
            cbt, cbb = u["cb"]
            lastmm = not u["diag"]
            mm(k, bank[:, c0:TG], k.sel33, cbt[0:33, c0:TG], False, lastmm, [cbb] + C, [bb], lastmm)
        if u["diag"]:
            mm(k, bank[:, c0:c0 + 128], k.ident, k.maskA, False, True, C, [bb], True)

    def actB(u):
        c0 = u["c0"]
        bank, bb = u["P2"]
        u["ab"] = ab.next()
        a_, abb = u["ab"]
        p.op("act", ("activation", dict(out=a_[:, c0:TG], in_=bank[:, c0:TG], func=AF.Exp)),
             reads=[bb], writes=[abb])

    def s3(u):
        hh, g, j, c0 = u["hh"], u["g"], u["j"], u["c0"]
        a_, abb = u["ab"]
        po, pob = u["PO"]
        vj = v_sb[:, hh, j * 128:(j + 1) * 128]
        mm(k, po[:, c0:TG], vj, a_[:, c0:TG], False, u["last"], [vb[hh], abb], [pob], True)
        if u["last"]:
            p.op("dve", ("tensor_copy", dict(out=q_sb[:, hh, g * TG:(g + 1) * TG], in_=po[:])),
                 reads=[pob], writes=[qb[hh][g]])

    s1(units[0])
    for i in range(n + 1):
        if i + 1 < n:
            s1(units[i + 1])
        if i < n:
            actA(units[i])
            s2(units[i], units[i + 1] if not units[i]["last"] else None)
        if i >= 1:
            actB(units[i - 1])
            s3(units[i - 1])


def build_p3():
    nc = bass.Bass("TRN2", target_bir_lowering=False)
    with ExitStack() as stack:
        k = K(nc, stack)
        p = k.p
        x_d = k.din("xT", [D, T])
        o_d = k.din("oT", [D, T], BF16)
        cm_d = k.din("cm", [128, 768], BF16)
        g_d = k.din("gains", [128, 8 * KC])
        gn_d = k.din("gn", [128, 4])
        cos_d = k.din("cosT", [128, T])
        sin_d = k.din("sinT", [128, T])
        wo_d = k.din("wo", [8, 128, 4096])
        wgu_d = k.din("wgu", [FC, 128, 4096])
        wd_d = k.din("wd", [NQ, 8, 128, 2048])
        wk_d = k.din("wk", [8, 128, 4096])
        wv_d = k.din("wv", [8, 128, 4096])
        wq_d = k.din("wq", [8, 128, 4096])
        x_o = k.dout("xo", [D, T])
        k_o = k.dout("kshT", [D, T], BF16)
        v_o = k.dout("vsh", [8, 128, 8, 256], BF16)
        q_o = k.dout("qbT", [D, T], BF16)
        k.consts(cm_d)
        st = TokState(k)
        p.dma("sp", st.gains[:], g_d, writes=[st.gains_b])
        load_x(k, st, x_d)
        for m in range(KC):
            p.dma("sp", st.qo[:, m, :], o_d[m * 128:(m + 1) * 128, :], writes=st.qob[m])
        linear_fm(k, st, wo_d, 8, 2, KC, lambda kc, tg: st.qo[:, kc, tg * TG:(tg + 1) * TG],
                  lambda kc, tg: [st.qob[kc][tg]], lambda m, tg, bank, bb: evac_resid(k, st, m, tg, bank, bb))
        ffn(k, st, wgu_d, wd_d, 16)
        rp = RopeState(k)
        p.dma("sp", rp.gn[:], gn_d, writes=[rp.gn_b])
        rope_tables(k, rp, cos_d, sin_d, 0)
        rmsnorm(k, st, 32)
        kst = st.stg
        cur = {}

        def mk_evac(dst, scale):
            def ev(m, tg, bank, bb):
                ts = slice(tg * TG, (tg + 1) * TG)
                if tg == 0:
                    cur["s"] = kst.next()
                s, sb_ = cur["s"]
                qknorm_rope_chunk(k, st, rp, bank, bb, tg, s[:, ts], [sb_], scale)
                if tg == 1:
                    p.dma("sp", dst[m * 128:(m + 1) * 128, :], s[:, 0:T], reads=[sb_], is_out=True)
            return ev

        hf = lambda kc, tg: st.h[:, kc, tg * TG:(tg + 1) * TG]
        hbf = lambda kc, tg: [st.hb[tg]]
        linear_fm(k, st, wk_d, 8, 2, KC, hf, hbf, mk_evac(k_o, 1.0))

        def out_v(l, vacc, vb):
            p.dma("sp", v_o[l], vacc, reads=[vb], is_out=True)

        linear_tm_v(k, st, wv_d, 8, 256, out_v)
        store_x(k, st, x_o)
        rope_tables(k, rp, cos_d, sin_d, 2)
        rmsnorm(k, st, 48)
        linear_fm(k, st, wq_d, 8, 2, KC, hf, hbf, mk_evac(q_o, SCALE))
        p.emit()
    return nc


def build_p4():
    nc = bass.Bass("TRN2", target_bir_lowering=False)
    with ExitStack() as stack:
        k = K(nc, stack)
        p = k.p
        cm_d = k.din("cm", [128, 768], BF16)
        q_d = k.din("qh", [4, 128, S], BF16)
        k_d = k.din("kh", [4, 128, S], BF16)
        v_d = k.din("vh", [2, 128, 32 * 256], BF16)
        lam_d = k.din("lamv", [128, 4])
        sl_d = k.din("subln", [128, 2])
        o_d = k.dout("oT", [4, 128, S], BF16)
        k.consts(cm_d)
        C = k.cbufs
        q_sb = k.sb([128, 4, S], BF16, "q_sb")
        k_sb = k.sb([128, 4, S], BF16, "k_sb")
        v_sb = k.sb([128, 2, 32 * 256], BF16, "v_sb")
        o_sb = k.sb([128, 4, S], BF16, "o_sb")
        qb = [Buf("q%d" % h) for h in range(4)]
        kb = [Buf("k%d" % h) for h in range(4)]
        vb = [Buf("v%d" % h) for h in range(2)]
        ob = [Buf("o%d" % h) for h in range(4)]
        for h in range(4):
            p.dma("sp", q_sb[:, h, :], q_d[h], writes=[qb[h]])
            p.dma("sp", k_sb[:, h, :], k_d[h], writes=[kb[h]])
        for h in range(2):
            p.dma("sp", v_sb[:, h, :], v_d[h], writes=[vb[h]])
        lam = k.sb([128, 4], F32, "lam")
        lam2 = k.sb([128, 4], F32, "lam2")
        sl = k.sb([128, 2], F32, "sl")
        lb = Buf("lam")
        slb = Buf("sl")
        onesf = k.sb([128, 128], F32, "onesf")
        p.dma("sp", lam[:], lam_d, writes=[lb])
        p.dma("sp", sl[:], sl_d, writes=[slb])
        p.op("dve", ("memset", dict(_pos=(onesf[:], 1.0))), writes=[lb])
        p.op("dve", ("tensor_tensor", dict(out=lam2[:, 0:1], in0=lam[:, 0:1], in1=lam[:, 1:2], op=ALU.mult)),
             reads=[lb], writes=[lb])
        p.op("dve", ("tensor_tensor", dict(out=lam2[:, 1:2], in0=lam[:, 2:3], in1=lam[:, 3:4], op=ALU.mult)),
             reads=[lb], writes=[lb])
        bank, bb = k.banks.next()
        mm(k, bank[:, 0:2], onesf[:], lam2[:, 0:2], True, True, [lb], [bb], True)
        p.op("act", ("activation", dict(out=lam2[:, 2:4], in_=bank[:, 0:2], func=AF.Exp)), reads=[bb], writes=[lb])
        p.op("dve", ("scalar_tensor_tensor", dict(out=lam[:, 0:1], in0=lam2[:, 3:4], scalar=-LAMBDA_INIT,
                                                     in1=lam2[:, 2:3], op0=ALU.add, op1=ALU.subtract)),
             reads=[lb], writes=[lb])
        neglam = lam[:, 0:1]
        p.op("dve", ("tensor_scalar", dict(out=sl[:], in0=sl[:], scalar1=1.0 - LAMBDA_INIT, scalar2=None,
                                              op0=ALU.mult)), reads=[slb], writes=[slb])

        allb = k.banks.items
        P1 = Ring(allb[0:3])
        PO = [allb[3], allb[4]]
        PL = allb[5]
        PX = Ring(allb[6:8])
        pb = Ring([(k.sb([128, TG], BF16, "pb%d" % i), Buf("pb%d" % i)) for i in range(3)])
        on = [(k.sb([128, 2, TG], F32, "on%d" % i), Buf("on%d" % i)) for i in range(2)]
        rl = (k.sb([128, TG], F32, "rl"), Buf("rl"))
        sqr = Ring([(k.sb([128, TG], BF16, "sqr%d" % i), Buf("sqr%d" % i)) for i in range(2)])
        rsd = (k.sb([128, TG], F32, "rsd"), Buf("rsd"))

        for hh in range(2):
            for g in range(8):
                for c in range(2):
                    hc = 2 * hh + c
                    units = []
                    for j in range(4 * g + 4):
                        c0 = max(0, (j - 4 * g) * 128)
                        units.append((j, c0, j >= 4 * g))
                    for (bk, bkb) in (PO[0], PO[1], PL):
                        mm(k, bk[:], k.ident, k.zeros[:], True, False, C, [bkb], True)
                    pend = None
                    for ui in range(len(units) + 1):
                        if ui < len(units):
                            j, c0, diag = units[ui]
                            bank, bb = P1.next()
                            mm(k, bank[:, c0:TG], k_sb[:, hc, j * 128:(j + 1) * 128],
                               q_sb[:, hc, g * TG + c0:(g + 1) * TG], True, not diag, [kb[hc], qb[hc]], [bb], not diag)
                            if diag:
                                mm(k, bank[:, c0:c0 + 128], k.ident, k.maskB, False, True, C, [bb], True)
                            pt, ptb = pb.next()
                            p.op("act", ("activation", dict(
                                out=pt[:, c0:TG], in_=bank[:, c0:TG], func=AF.Exp)), reads=[bb], writes=[ptb])
                            cur = (j, c0, pt, ptb)
                        else:
                            cur = None
                        if pend is not None:
                            j, c0, pt, ptb = pend
                            last = cur is None
                            for half in range(2):
                                vj = v_sb[:, hh, j * 256 + half * 128: j * 256 + half * 128 + 128]
                                mm(k, PO[half][0][:, c0:TG], vj, pt[:, c0:TG], False, last, [vb[hh], ptb],
                                   [PO[half][1]], True)
                            mm(k, PL[0][:, c0:TG], k.ones, pt[:, c0:TG], False, last, [ptb] + C, [PL[1]], True)
                        pend = cur
                    p.op("dve", ("reciprocal", dict(out=rl[0][:], in_=PL[0][:])), reads=[PL[1]], writes=[rl[1]])
                    for half in range(2):
                        p.op("dve", ("tensor_tensor", dict(
                            out=on[c][0][:, half, :], in0=PO[half][0][:], in1=rl[0][:], op=ALU.mult)),
                             reads=[PO[half][1], rl[1]], writes=[on[c][1]])
                o_, o_b = on[0]
                p.op("dve", ("scalar_tensor_tensor", dict(out=o_[:], in0=on[1][0][:], scalar=neglam, in1=o_[:],
                                                             op0=ALU.mult, op1=ALU.add)),
                     reads=[on[1][1], o_b, lb], writes=[o_b])
                bx, bxb = PX.next()
                for half in range(2):
                    sq, sqb = sqr.next()
                    p.op("act", ("activation", dict(out=sq[:], in_=o_[:, half, :],
                                                                         func=AF.Square)), reads=[o_b], writes=[sqb])
                    mm(k, bx[:], k.ones, sq[:], half == 0, half == 1, [sqb] + C, [bxb], True)
                p.op("act", ("activation", dict(out=rsd[0][:], in_=bx[:], func=AF.Sqrt, bias=k.eps,
                                                          scale=1.0 / 256)), reads=[bxb] + C, writes=[rsd[1]])
                p.op("dve", ("reciprocal", dict(out=rsd[0][:], in_=rsd[0][:])), reads=[rsd[1]], writes=[rsd[1]])
                for half in range(2):
                    p.op("dve", ("scalar_tensor_tensor", dict(
                        out=o_sb[:, 2 * hh + half, g * TG:(g + 1) * TG], in0=o_[:, half, :],
                        scalar=sl[:, half:half + 1], in1=rsd[0][:], op0=ALU.mult, op1=ALU.mult)),
                         reads=[o_b, rsd[1], slb], writes=[ob[2 * hh + half]])
        for h in range(4):
            p.dma("sp", o_d[h], o_sb[:, h, :], reads=[ob[h]], is_out=True)
        p.emit()
    return nc


def build_p5():
    nc = bass.Bass("TRN2", target_bir_lowering=False)
    with ExitStack() as stack:
        k = K(nc, stack)
        p = k.p
        x_d = k.din("xT", [D, T])
        o_d = k.din("oT", [D, T], BF16)
        cm_d = k.din("cm", [128, 768], BF16)
        g_d = k.din("gains", [128, 8 * KC])
        wo_d = k.din("wo", [8, 128, 4096])
        wgu_d = k.din("wgu", [FC, 128, 4096])
        wd_d = k.din("wd", [NQ, 8, 128, 2048])
        x_o = k.dout("xo", [D, T])
        k.consts(cm_d)
        st = TokState(k)
        p.dma("sp", st.gains[:], g_d, writes=[st.gains_b])
        load_x(k, st, x_d)
        for m in range(KC):
            p.dma("sp", st.qo[:, m, :], o_d[m * 128:(m + 1) * 128, :], writes=st.qob[m])
        linear_fm(k, st, wo_d, 8, 2, KC, lambda kc, tg: st.qo[:, kc, tg * TG:(tg + 1) * TG],
                  lambda kc, tg: [st.qob[kc][tg]], lambda m, tg, bank, bb: evac_resid(k, st, m, tg, bank, bb))
        ffn(k, st, wgu_d, wd_d, 64)
        store_x(k, st, x_o)
        p.emit()
    return nc


def w_fm(w, cpl):
    kd, n = w.shape
    kc = kd // 128
    a = w.reshape(kc, 128, n // 128, 128)
    a = a.transpose(2, 1, 0, 3)
    a = a.reshape(n // 128 // cpl, cpl, 128, kc, 128).transpose(0, 2, 1, 3, 4)
    return np.ascontiguousarray(a.reshape(n // 128 // cpl, 128, cpl * kc * 128))


def w_tm(w, ncol):
    kd, n = w.shape
    kc = kd // 128
    a = w.reshape(kc, 128, n // ncol, ncol).transpose(2, 1, 0, 3)
    return np.ascontiguousarray(a.reshape(n // ncol, 128, kc * ncol))


def w_gu(w):
    g = w_fm(w[:, :FF], 1)
    u = w_fm(w[:, FF:], 1)
    return np.ascontiguousarray(np.concatenate([g, u], axis=2))


def w_down(w):
    out = np.zeros((NQ, 8, 128, 2, 8, 128), np.float32)
    f0 = 0
    for q in range(NQ):
        fq = FQS[q]
        a = w_fm(w[f0 * 128:(f0 + fq) * 128, :], 2).reshape(8, 128, 2, fq, 128)
        out[q][:, :, :, :fq, :] = a
        f0 += fq
    return out.reshape(NQ, 8, 128, 2048)


def gain_fm(g):
    return np.ascontiguousarray(g.reshape(KC, 128).T)


def const_mats():
    i = np.arange(128)
    ident = np.eye(128, dtype=np.float32)
    negtri = -(i[:, None] >= i[None, :]).astype(np.float32)
    sel = np.zeros((128, 128), np.float32)
    sel[0, :] = 1.0
    sel[32, :] = 1.0
    maskA = np.where(i[:, None] >= i[None, :], NEG, 0.0).astype(np.float32)
    maskB = np.where(i[:, None] > i[None, :], NEG, 0.0).astype(np.float32)
    ones = np.ones((128, 128), np.float32)
    return np.concatenate([ident, negtri, sel, maskA, maskB, ones], axis=1).astype(BF)


def rope_tabs(pos):
    inv = ROPE_THETA ** (-np.arange(0, 128, 2, dtype=np.float32) / np.float32(128))
    ang = pos.astype(np.float32)[:, None] * inv.astype(np.float32)[None, :]
    ang = np.concatenate([ang, ang], axis=-1).astype(np.float32)
    cos = np.cos(ang).astype(np.float32).T
    sin = np.sin(ang).astype(np.float32).T.copy()
    sin[:64] *= -1.0
    return np.ascontiguousarray(cos), np.ascontiguousarray(sin)


_CACHE = {}


def _prog(name, fn):
    if name not in _CACHE:
        _CACHE[name] = fn()
    return _CACHE[name]


def _run(name, fn, in_maps):
    nc = _prog(name, fn)
    res = run_bass_kernel_spmd(nc, in_maps, core_ids=list(range(8)))
    return res.results


def kernel(x, a_attn_norm, a_w_qkv, a_w_o, a_ffn_norm, a_w_gate_up, a_w_down, kv_norm, w_kv, k_norm,
           b_attn_norm, b_w_q, b_q_norm, b_lambda_q1, b_lambda_k1, b_lambda_q2, b_lambda_k2,
           b_subln, b_w_o, b_ffn_norm, b_w_gate_up, b_w_down):
    f = lambda a: np.asarray(a, dtype=np.float32)
    x = f(x)
    cm = const_mats()
    gains = np.zeros((128, 8 * KC), np.float32)
    for i, g in enumerate([a_attn_norm[0], a_ffn_norm[0], kv_norm, b_attn_norm[0], b_ffn_norm[0]]):
        gains[:, i * KC:(i + 1) * KC] = gain_fm(f(g))
    kn = f(k_norm)
    qn = f(b_q_norm[0])
    gn = np.stack([kn, np.roll(kn, 64), qn, np.roll(qn, 64)], axis=1).astype(np.float32)
    cores = [(b, r) for b in range(2) for r in range(4)]
    xT = [np.ascontiguousarray(x[b, r * T:(r + 1) * T, :].T) for (b, r) in cores]

    wqkv = f(a_w_qkv[0])
    wqk = w_fm(wqkv[:, :2 * D], 2)
    wv = w_tm(wqkv[:, 2 * D:], 256)
    r1 = _run("p1", build_p1, [dict(xT=xT[i], cm=cm, gains=gains, wqk=wqk, wv=wv) for i in range(8)])
    del wqkv, wqk, wv

    in2 = []
    for (b, j) in cores:
        rs = [r1[b * 4 + r] for r in range(4)]
        qh = np.concatenate([np.asarray(rr["qT"])[512 * j:512 * j + 512, :] for rr in rs], axis=1).reshape(4, 128, S)
        kh = np.concatenate([np.asarray(rr["kT"])[512 * j:512 * j + 512, :] for rr in rs], axis=1).reshape(4, 128, S)
        vh = np.concatenate([np.asarray(rr["v"])[4 * j:4 * j + 4] for rr in rs], axis=2).reshape(4, 128, S)
        in2.append(dict(cm=cm, qh=np.ascontiguousarray(qh), kh=np.ascontiguousarray(kh),
                        vh=np.ascontiguousarray(vh)))
    r2 = _run("p2", build_p2, in2)
    del in2

    def gather_o(res):
        outs = []
        for (b, r) in cores:
            o = np.concatenate([np.asarray(res[b * 4 + j]["oT"]).reshape(512, S)[:, r * T:(r + 1) * T]
                                for j in range(4)], axis=0)
            outs.append(np.ascontiguousarray(o))
        return outs

    oA = gather_o(r2)

    wkv = f(w_kv)
    tabs = [rope_tabs(np.arange(r * T, (r + 1) * T)) for r in range(4)]
    com3 = dict(cm=cm, gains=gains, gn=gn, wo=w_fm(f(a_w_o[0]), 2), wgu=w_gu(f(a_w_gate_up[0])),
                wd=w_down(f(a_w_down[0])), wk=w_fm(wkv[:, :D], 2), wv=w_tm(wkv[:, D:], 256),
                wq=w_fm(f(b_w_q[0]), 2))
    in3 = []
    for i, (b, r) in enumerate(cores):
        d = dict(com3)
        d.update(xT=xT[i], oT=oA[i], cosT=tabs[r][0], sinT=tabs[r][1])
        in3.append(d)
    r3 = _run("p3", build_p3, in3)
    del in3, com3

    lamv = np.stack([np.asarray(v, np.float32).reshape(128) for v in
                     (b_lambda_q1[0], b_lambda_k1[0], b_lambda_q2[0], b_lambda_k2[0])], axis=1)
    subln = np.ascontiguousarray(f(b_subln[0]).reshape(2, 128).T)
    in4 = []
    for (b, j) in cores:
        rs = [r3[b * 4 + r] for r in range(4)]
        qh = np.concatenate([np.asarray(rr["qbT"])[512 * j:512 * j + 512, :] for rr in rs], axis=1).reshape(4, 128, S)
        kh = np.concatenate([np.asarray(rr["kshT"])[512 * j:512 * j + 512, :] for rr in rs], axis=1).reshape(4, 128, S)
        vh = np.concatenate([np.asarray(rr["vsh"])[2 * j:2 * j + 2] for rr in rs], axis=2).reshape(2, 128, 32 * 256)
        in4.append(dict(cm=cm, qh=np.ascontiguousarray(qh), kh=np.ascontiguousarray(kh),
                        vh=np.ascontiguousarray(vh), lamv=np.ascontiguousarray(lamv), subln=subln))
    r4 = _run("p4", build_p4, in4)
    del in4
    oB = gather_o(r4)

    com5 = dict(cm=cm, gains=gains, wo=w_fm(f(b_w_o[0]), 2), wgu=w_gu(f(b_w_gate_up[0])),
                wd=w_down(f(b_w_down[0])))
    in5 = []
    for i in range(8):
        d = dict(com5)
        d.update(xT=np.asarray(r3[i]["xo"]), oT=oB[i])
        in5.append(d)
    r5 = _run("p5", build_p5, in5)
    out = np.empty((2, S, D), np.float32)
    for i, (b, r) in enumerate(cores):
        out[b, r * T:(r + 1) * T, :] = np.asarray(r5[i]["xo"]).T
    return out
```

```python
import math
from contextlib import ExitStack

import numpy as np
import ml_dtypes

import concourse.bass as bass
import concourse.mybir as mybir
from concourse.bass_utils import run_bass_kernel_spmd

F32 = mybir.dt.float32
BF16 = mybir.dt.bfloat16
AF = mybir.ActivationFunctionType
ALU = mybir.AluOpType
BF = ml_dtypes.bfloat16

D = 2048
KC = 16
T = 1024
S = 4096
NTG = 2
TG = 512
FF = 5632
FC = 44
FQS = [8, 8, 7, 7, 7, 7]
NQ = len(FQS)
EPS = 1e-6
NEG = -30000.0
ROPE_THETA = 10000.0
LAMBDA_INIT = 0.8 - 0.6 * math.exp(-0.3 * 1)
SCALE = 128.0 ** -0.5
NDS = 12


class Buf:
    __slots__ = ("name", "w", "r")

    def __init__(self, name):
        self.name = name
        self.w = None
        self.r = []


class Prog:
    ENG = ("pe", "act", "dve", "pool", "sp")

    def __init__(self, nc, stack):
        self.nc = nc
        self.q = {e: [] for e in self.ENG}
        self.sem = {e: stack.enter_context(nc.semaphore("pg_" + e)) for e in self.ENG}
        self.cnt = dict.fromkeys(self.ENG, 0)
        self.waited = {e: {} for e in self.ENG}
        self.pend_r = {e: [] for e in self.ENG}
        self.pend_w = {e: [] for e in self.ENG}
        self.dsem = {e: [stack.enter_context(nc.semaphore("d_%s%d" % (e, i))) for i in range(NDS)]
                     for e in ("sp", "pool")}
        self.dcnt = {e: [0] * NDS for e in ("sp", "pool")}
        self.dnext = {"sp": 0, "pool": 0}
        self.out_toks = []
        self.ccsem = stack.enter_context(nc.semaphore("cc_sem"))
        self.ccnt = 0

    def _wait(self, eng, tok):
        sem, val, src = tok
        if src == "pe" and eng == "pe":
            return
        key = id(sem)
        if self.waited[eng].get(key, 0) >= val:
            return
        self.waited[eng][key] = val
        self.q[eng].append(lambda e, sem=sem, val=val: e.wait_ge(sem, val))

    def _deps(self, eng, reads, writes):
        for b in reads:
            if b.w is not None:
                self._wait(eng, b.w)
        for b in writes:
            if b.w is not None:
                self._wait(eng, b.w)
            for t in b.r:
                self._wait(eng, t)

    def op(self, eng, call, reads=(), writes=(), mark=True):
        name, kw = call
        pos = kw.pop("_pos", ())
        fn = (lambda e, name=name, kw=kw, pos=pos: getattr(e, name)(*pos, **kw))
        self._deps(eng, reads, writes)
        if not mark:
            self.q[eng].append(lambda e, fn=fn: fn(e))
            self.pend_r[eng].extend(reads)
            self.pend_w[eng].extend(writes)
            return None
        self.cnt[eng] += 1
        sem = self.sem[eng]
        tok = (sem, self.cnt[eng], eng)
        self.q[eng].append(lambda e, fn=fn, sem=sem: fn(e).then_inc(sem, 1))
        for b in self.pend_r[eng]:
            b.r.append(tok)
        for b in self.pend_w[eng]:
            b.w = tok
            b.r = []
        self.pend_r[eng] = []
        self.pend_w[eng] = []
        for b in reads:
            b.r.append(tok)
        for b in writes:
            b.w = tok
            b.r = []
        return tok

    def dma(self, eng, out, in_, reads=(), writes=(), is_out=False):
        i = self.dnext[eng]
        self.dnext[eng] = (i + 1) % NDS
        sem = self.dsem[eng][i]
        prev = self.dcnt[eng][i]
        if prev > 0:
            self._wait(eng, (sem, prev, "dma"))
        self._deps(eng, reads, writes)
        self.dcnt[eng][i] = prev + 16
        tok = (sem, prev + 16, "dma")
        self.q[eng].append(lambda e, out=out, in_=in_, sem=sem: e.dma_start(out=out, in_=in_).then_inc(sem, 16))
        for b in reads:
            b.r.append(tok)
        for b in writes:
            b.w = tok
            b.r = []
        if is_out:
            self.out_toks.append(tok)
        return tok

    def gather(self, out, in_, idx_ap, elem_off, reads=(), writes=()):
        eng = "pool"
        i = self.dnext[eng]
        self.dnext[eng] = (i + 1) % NDS
        sem = self.dsem[eng][i]
        prev = self.dcnt[eng][i]
        if prev > 0:
            self._wait(eng, (sem, prev, "dma"))
        self._deps(eng, reads, writes)
        self.dcnt[eng][i] = prev + 16
        tok = (sem, prev + 16, "dma")
        self.q[eng].append(lambda e, out=out, in_=in_, idx_ap=idx_ap, elem_off=elem_off, sem=sem: e.indirect_dma_start(
            out=out, out_offset=None, in_=in_, in_offset=bass.IndirectOffsetOnAxis(ap=idx_ap, axis=0),
            element_offset=elem_off).then_inc(sem, 16))
        for b in reads:
            b.r.append(tok)
        for b in writes:
            b.w = tok
            b.r = []
        return tok

    def allgather(self, in_ap, out_ap, groups, reads=(), writes=()):
        eng = "pool"
        self._deps(eng, reads, writes)
        self.ccnt += 1
        sem = self.ccsem
        tok = (sem, self.ccnt, "dma")
        self.q[eng].append(lambda e, in_ap=in_ap, out_ap=out_ap, sem=sem: e.collective_compute(
            "AllGather", ALU.bypass, replica_groups=groups, ins=[in_ap], outs=[out_ap]).then_inc(sem))
        for b in reads:
            b.r.append(tok)
        for b in writes:
            b.w = tok
            b.r = []
        return tok

    def barrier(self):
        for e in self.ENG:
            assert not self.pend_r[e] and not self.pend_w[e], "barrier inside an open PE group"
        for e in self.ENG:
            for e2 in self.ENG:
                if e2 != e and self.cnt[e2] > 0:
                    self._wait(e, (self.sem[e2], self.cnt[e2], e2))
            for qn in ("sp", "pool"):
                for i in range(NDS):
                    if self.dcnt[qn][i] > 0:
                        self._wait(e, (self.dsem[qn][i], self.dcnt[qn][i], "dma"))

    def emit(self):
        for tok in self.out_toks:
            self._wait("sp", tok)
        for e in ("pe", "act", "dve", "pool"):
            if self.cnt[e] > 0:
                self._wait("sp", (self.sem[e], self.cnt[e], e))
        for e in ("sp", "pool"):
            for i in range(NDS):
                if self.dcnt[e][i] > 0:
                    self._wait("sp", (self.dsem[e][i], self.dcnt[e][i], "dma"))
        q = self.q
        with self.nc.Block() as block:
            @block.tensor
            def _(e):
                for f in q["pe"]:
                    f(e)

            @block.scalar
            def _(e):
                for f in q["act"]:
                    f(e)

            @block.vector
            def _(e):
                for f in q["dve"]:
                    f(e)

            @block.gpsimd
            def _(e):
                for f in q["pool"]:
                    f(e)

            @block.sync
            def _(e):
                for f in q["sp"]:
                    f(e)


class Ring:
    def __init__(self, items):
        self.items = items
        self.i = 0

    def next(self):
        it = self.items[self.i]
        self.i = (self.i + 1) % len(self.items)
        return it


class K:
    def __init__(self, nc, stack):
        self.nc = nc
        self.stack = stack
        self.p = Prog(nc, stack)
        self.nalloc = 0
        self.banks = Ring([(self.psum("bank%d" % i), Buf("bank%d" % i)) for i in range(8)])

    def sb(self, shape, dtype, name=None):
        self.nalloc += 1
        t = self.stack.enter_context(self.nc.sbuf_tensor("sb_" + (name or ("t%d" % self.nalloc)), list(shape), dtype))
        return t

    def psum(self, name):
        return self.stack.enter_context(self.nc.psum_tensor("ps_" + name, [128, 512], F32))

    def din(self, name, shape, dtype=F32):
        return self.nc.dram_tensor(name, list(shape), dtype, kind="ExternalInput").ap()

    def dout(self, name, shape, dtype=F32):
        return self.nc.dram_tensor(name, list(shape), dtype, kind="ExternalOutput").ap()

    def consts(self, cm_ap):
        p = self.p
        self.cm = self.sb([128, 6 * 128], BF16, "cm")
        self.cm_b = Buf("cm")
        p.dma("sp", self.cm[:], cm_ap, writes=[self.cm_b])
        self.ident = self.cm[:, 0:128]
        self.negtri = self.cm[:, 128:256]
        self.sel33 = self.cm[0:33, 256:384]
        self.maskA = self.cm[:, 384:512]
        self.maskB = self.cm[:, 512:640]
        self.ones = self.cm[:, 640:768]
        self.cf = self.sb([128, 4], F32, "cf")
        self.cf_b = Buf("cf")
        p.op("dve", ("memset", dict(_pos=(self.cf[:, 0:1], EPS))), writes=[self.cf_b])
        p.op("dve", ("memset", dict(_pos=(self.cf[:, 1:2], 1.0))), writes=[self.cf_b])
        p.op("dve", ("memset", dict(_pos=(self.cf[:, 2:3], 0.0))), writes=[self.cf_b])
        self.eps = self.cf[:, 0:1]
        self.one = self.cf[:, 1:2]
        self.zero = self.cf[:, 2:3]
        self.negones = self.sb([128, 128], BF16, "negones")
        self.zeros = self.sb([128, 512], BF16, "zeros")
        self.cz_b = Buf("cz")
        p.op("dve", ("memset", dict(_pos=(self.negones[:], -1.0))), writes=[self.cz_b])
        p.op("dve", ("memset", dict(_pos=(self.zeros[:], 0.0))), writes=[self.cz_b])
        self.cbufs = [self.cm_b, self.cf_b, self.cz_b]


def mm(k, out, lhsT, rhs, start, stop, reads, writes, mark):
    return k.p.op("pe", ("matmul", dict(_pos=(out,), lhsT=lhsT, rhs=rhs, start=start, stop=stop)),
                  reads=reads, writes=writes, mark=mark)


class Mem:
    def __init__(self, k):
        self.k = k
        self.x = k.sb([128, KC, T], F32, "xT")
        self.xb = [[Buf("x%d_%d" % (m, tg)) for tg in range(NTG)] for m in range(KC)]
        self.A1 = k.sb([128, KC * T], BF16, "A1")
        self.A2 = k.sb([128, KC * T], BF16, "A2")
        self.A3 = k.sb([128, 8192], F32, "A3")
        self.att = k.sb([128, 3072], F32, "att")
        self.wslots = Ring([(k.sb([128, 4096], BF16, "w%d" % i), Buf("w%d" % i)) for i in range(3)])
        self.sq = Ring([(k.sb([128, TG], BF16, "sq%d" % i), Buf("sq%d" % i)) for i in range(3)])
        self.rstd = Ring([(k.sb([128, TG], F32, "rstd%d" % i), Buf("rstd%d" % i)) for i in range(2)])
        self.gains = k.sb([128, 8 * KC], F32, "gains")
        self.gains_b = Buf("gains")
        self.gn = k.sb([128, 4], F32, "gn")
        self.gn_b = Buf("gn")
        self.idx = k.sb([128, 4], mybir.dt.uint32, "idx")
        self.idx_b = Buf("idx")

    def v3(self, ap, n):
        return ap[:, :].rearrange("p (k t) -> p k t", k=n)


class TokState:
    def __init__(self, k, mem):
        self.k = k
        self.mem = mem
        self.x = mem.x
        self.xb = mem.xb
        self.h = mem.v3(mem.A1, KC)
        self.hb = [Buf("h%d" % tg) for tg in range(NTG)]
        self.qo = mem.v3(mem.A2, KC)
        self.qob = [[Buf("qo%d_%d" % (m, tg)) for tg in range(NTG)] for m in range(KC)]
        self.wslots = mem.wslots
        self.sq = mem.sq
        self.rstd = mem.rstd
        A3 = mem.A3
        self.tmpf = Ring([(A3[:, i * 512:(i + 1) * 512], Buf("tmpf%d" % i)) for i in range(4)])
        self.stg = Ring([(A3[:, 2048 + i * 1024:2048 + (i + 1) * 1024].bitcast(BF16), Buf("stg%d" % i))
                         for i in range(2)])
        self.gains = mem.gains
        self.gains_b = mem.gains_b
        self.keep_rstd = [(A3[:, 7168 + i * 512:7168 + (i + 1) * 512], Buf("krstd%d" % i)) for i in range(2)]
        self.evac_i = 0
        self.deferred = []

    def wload(self, dram_ap, n):
        k = self.k
        slot, b = self.wslots.next()
        k.p.dma("pool", slot[:, 0:n], dram_ap, writes=[b])
        self.tick()
        return slot, b

    def defer(self, fn, delay=2):
        self.deferred.append([delay, fn])

    def tick(self):
        keep = []
        for it in self.deferred:
            it[0] -= 1
            if it[0] <= 0:
                it[1]()
            else:
                keep.append(it)
        self.deferred = keep

    def flush(self):
        for it in self.deferred:
            it[1]()
        self.deferred = []


def load_x(k, st, x_dram):
    for kc in range(KC):
        k.p.dma("sp", st.x[:, kc, :], x_dram[kc * 128:(kc + 1) * 128, :], writes=st.xb[kc])


def store_x(k, st, x_dram):
    for kc in range(KC):
        k.p.dma("sp", x_dram[kc * 128:(kc + 1) * 128, :], st.x[:, kc, :], reads=st.xb[kc], is_out=True)


def rmsnorm(k, st, gcol, keep=False, reuse=False):
    p = k.p
    for tg in range(NTG):
        ts = slice(tg * TG, (tg + 1) * TG)
        if reuse:
            rs, rsb = st.keep_rstd[tg]
        else:
            bank, bb = k.banks.next()
            for kc in range(KC):
                sq, sqb = st.sq.next()
                if kc % 2 == 0:
                    p.op("act", ("activation", dict(out=sq[:], in_=st.x[:, kc, ts], func=AF.Square)),
                         reads=[st.xb[kc][tg]], writes=[sqb])
                else:
                    p.op("dve", ("tensor_tensor", dict(out=sq[:], in0=st.x[:, kc, ts], in1=st.x[:, kc, ts],
                                                       op=ALU.mult)), reads=[st.xb[kc][tg]], writes=[sqb])
                mm(k, bank[:], k.ones, sq[:], kc == 0, kc == KC - 1, [sqb] + k.cbufs, [bb], True)
            rs, rsb = st.keep_rstd[tg] if keep else st.rstd.next()
            p.op("act", ("activation", dict(out=rs[:], in_=bank[:], func=AF.Ln, bias=k.eps, scale=1.0 / D)),
                 reads=[bb] + k.cbufs, writes=[rsb])
            p.op("act", ("activation", dict(out=rs[:], in_=rs[:], func=AF.Exp, scale=-0.5)),
                 reads=[rsb], writes=[rsb])
        for kc in range(KC):
            p.op("dve", ("scalar_tensor_tensor", dict(
                out=st.h[:, kc, ts], in0=st.x[:, kc, ts], scalar=st.gains[:, gcol + kc:gcol + kc + 1], in1=rs[:],
                op0=ALU.mult, op1=ALU.mult)),
                 reads=[st.xb[kc][tg], rsb, st.gains_b], writes=[st.hb[tg]])


def linear_fm(k, st, w_dram, nload, cpl, nkc, rhs_fn, rhs_bufs_fn, evac, kpad=None, pre=()):
    kp = kpad or nkc
    pending = None
    for l in range(nload):
        if l < len(pre):
            slot, wb = pre[l]
        else:
            slot, wb = st.wload(w_dram[l], cpl * kp * 128)
        for ci in range(cpl):
            m = l * cpl + ci
            for tg in range(NTG):
                bank, bb = k.banks.next()
                for kc in range(nkc):
                    o = (ci * kp + kc) * 128
                    mm(k, bank[:], slot[:, o:o + 128], rhs_fn(kc, tg), kc == 0, kc == nkc - 1,
                       [wb] + rhs_bufs_fn(kc, tg), [bb], kc == nkc - 1)
                if pending is not None:
                    pending()
                    pending = None
                r_ = evac(m, tg, bank, bb)
                if callable(r_):
                    pending = r_
    if pending is not None:
        pending()


def evac_scaled(k, st, out_ap, bank, bb, scale, wbufs):
    st.evac_i += 1
    if st.evac_i % 2 == 0:
        k.p.op("act", ("activation", dict(out=out_ap, in_=bank[:], func=AF.Copy, scale=scale)),
               reads=[bb], writes=wbufs)
    else:
        k.p.op("dve", ("tensor_scalar", dict(out=out_ap, in0=bank[:], scalar1=scale, scalar2=None, op0=ALU.mult)),
               reads=[bb], writes=wbufs)


def evac_resid(k, st, m, tg, bank, bb):
    ts = slice(tg * TG, (tg + 1) * TG)
    k.p.op("dve", ("tensor_tensor", dict(out=st.x[:, m, ts], in0=st.x[:, m, ts], in1=bank[:], op=ALU.add)),
           reads=[bb], writes=[st.xb[m][tg]])


def linear_tm_v(k, st, w_dram, nload, ncol, out_fn):
    p = k.p
    for l in range(nload):
        slot, wb = st.wload(w_dram[l], KC * ncol)
        vt, vb = st.stg.next()
        vacc = vt.rearrange("p (t c) -> p t c", c=ncol)
        for tb in range(8):
            tg = tb // 4
            bank, bb = k.banks.next()
            for kc in range(KC):
                mm(k, bank[:, 0:ncol], st.h[:, kc, tb * 128:(tb + 1) * 128], slot[:, kc * ncol:(kc + 1) * ncol],
                   kc == 0, kc == KC - 1, [wb, st.hb[tg]], [bb], kc == KC - 1)
            st.evac_i += 1
            if st.evac_i % 2 == 0:
                p.op("act", ("activation", dict(
                    out=vacc[:, tb, :], in_=bank[:, 0:ncol], func=AF.Copy)), reads=[bb], writes=[vb])
            else:
                p.op("dve", ("tensor_copy", dict(
                    out=vacc[:, tb, :], in_=bank[:, 0:ncol])), reads=[bb], writes=[vb])
        out_fn(l, vacc, vb)


def ffn(k, st, wgu, wd, gcol):
    p = k.p
    rmsnorm(k, st, gcol)
    acts = [(st.qo[:, 0:8, :], Buf("act0")), (st.qo[:, 8:16, :], Buf("act1"))]
    for a in range(2):
        for m in range(8 * a, 8 * a + 8):
            for tg in range(NTG):
                b = st.qob[m][tg]
                if b.w is not None:
                    acts[a][1].r.append(b.w)
                acts[a][1].r.extend(b.r)
    f0 = 0
    for q in range(NQ):
        fq = FQS[q]
        act, ab = acts[q % 2]
        for f in range(fq):
            slot, wb = st.wload(wgu[f0 + f], 2 * KC * 128)
            for tg in range(NTG):
                ts = slice(tg * TG, (tg + 1) * TG)
                bg, bgb = k.banks.next()
                bu, bub = k.banks.next()
                for kc in range(KC):
                    mm(k, bg[:], slot[:, kc * 128:(kc + 1) * 128], st.h[:, kc, ts], kc == 0, kc == KC - 1,
                       [wb, st.hb[tg]], [bgb], kc == KC - 1)
                for kc in range(KC):
                    o = (KC + kc) * 128
                    mm(k, bu[:], slot[:, o:o + 128], st.h[:, kc, ts], kc == 0, kc == KC - 1,
                       [wb, st.hb[tg]], [bub], kc == KC - 1)
                sg, sgb = st.tmpf.next()
                p.op("act", ("activation", dict(out=sg[:], in_=bg[:], func=AF.Silu)),
                     reads=[bgb], writes=[sgb])
                p.op("dve", ("tensor_tensor", dict(
                    out=act[:, f, ts], in0=sg[:], in1=bu[:], op=ALU.mult)),
                     reads=[sgb, bub], writes=[ab])
        f0 += fq
        linear_fm(k, st, wd[q], 8, 2, fq,
                  lambda kc, tg, act=act: act[:, kc, tg * TG:(tg + 1) * TG],
                  lambda kc, tg, ab=ab: [ab],
                  lambda m, tg, bank, bb: evac_resid(k, st, m, tg, bank, bb), kpad=8)


def qknorm_rope_chunk(k, st, rp, bank, bb, tg, out_ap, out_bufs, extra_scale):
    p = k.p
    ts = slice(tg * TG, (tg + 1) * TG)
    raw, rawb = st.tmpf.next()
    p.op("act", ("activation", dict(out=raw[:], in_=bank[:], func=AF.Copy)), reads=[bb], writes=[rawb])
    sq, sqb = st.sq.next()
    p.op("act", ("activation", dict(out=sq[:], in_=raw[:], func=AF.Square)), reads=[rawb], writes=[sqb])
    t1, t1b = st.tmpf.next()
    p.op("dve", ("tensor_tensor", dict(out=t1[:], in0=raw[:], in1=rp.cg[:, ts], op=ALU.mult)),
         reads=[rawb, rp.tab_b], writes=[t1b])

    def cont():
        bs, bsb = k.banks.next()
        mm(k, bs[:], rp.perm[:], raw[:], True, True, [rawb, rp.perm_b], [bsb], True)
        b2, b2b = k.banks.next()
        mm(k, b2[:], k.ones, sq[:], True, True, [sqb] + k.cbufs, [b2b], True)
        rs, rsb = st.rstd.next()
        p.op("act", ("activation", dict(out=rs[:], in_=b2[:], func=AF.Ln, bias=k.eps, scale=1.0 / 128)),
             reads=[b2b] + k.cbufs, writes=[rsb])
        p.op("act", ("activation", dict(out=rs[:], in_=rs[:], func=AF.Exp, scale=-0.5)), reads=[rsb], writes=[rsb])
        sw, swb = rp.sw.next()
        p.op("dve", ("tensor_tensor", dict(out=sw[:], in0=bs[:], in1=rp.sg[:, ts], op=ALU.mult)),
             reads=[bsb, rp.tab_b], writes=[swb])
        p.op("dve", ("tensor_tensor", dict(out=t1[:], in0=t1[:], in1=sw[:], op=ALU.add)),
             reads=[t1b, swb], writes=[t1b])
        p.op("dve", ("scalar_tensor_tensor", dict(out=out_ap, in0=t1[:], scalar=float(extra_scale), in1=rs[:],
                                                     op0=ALU.mult, op1=ALU.mult)),
             reads=[t1b, rsb], writes=out_bufs)
    return cont


class RopeState:
    def __init__(self, k, mem):
        A3 = mem.A3
        self.sw = Ring([(A3[:, 4096 + i * 512:4096 + (i + 1) * 512], Buf("sw%d" % i)) for i in range(2)])
        self.cg = A3[:, 5120:6144]
        self.sg = A3[:, 6144:7168]
        self.tab_b = Buf("ropetab")
        self.gn = mem.gn
        self.gn_b = mem.gn_b
        self.perm = mem.perm
        self.perm_b = mem.perm_b


def rope_tables(k, rp, cos_d, sin_d, col):
    p = k.p
    p.dma("sp", rp.cg, cos_d, writes=[rp.tab_b])
    p.dma("sp", rp.sg, sin_d, writes=[rp.tab_b])
    p.op("dve", ("tensor_scalar", dict(out=rp.cg, in0=rp.cg, scalar1=rp.gn[:, col:col + 1], scalar2=None,
                                          op0=ALU.mult)), reads=[rp.gn_b, rp.tab_b], writes=[rp.tab_b])
    p.op("dve", ("tensor_scalar", dict(out=rp.sg, in0=rp.sg, scalar1=rp.gn[:, col + 1:col + 2],
                                          scalar2=None, op0=ALU.mult)), reads=[rp.gn_b, rp.tab_b], writes=[rp.tab_b])


def attn_a_units():
    units = []
    for hh in range(4):
        for g in range(8):
            jmax = 4 * g + 3
            for j in range(jmax, -1, -1):
                c0 = max(0, (j - 4 * g) * 128)
                units.append(dict(hh=hh, g=g, j=j, c0=c0, first=(j == jmax), last=(j == 0), diag=(j >= 4 * g)))
    return units


def attn_a(k, mem, q_sb, k_sb, v_sb, qb, kb, vb, on_head_done=None, on_group_done=None):
    p = k.p
    allb = k.banks.items
    P1 = Ring(allb[0:2])
    P2 = Ring(allb[2:4])
    PO = Ring(allb[4:6])
    PC = Ring(allb[6:8])
    att = mem.att
    eb = Ring([(att[:, i * 512:(i + 1) * 512], Buf("eb%d" % i)) for i in range(2)])
    spb = Ring([(att[:, 1024 + i * 256:1024 + (i + 1) * 256].bitcast(BF16), Buf("sp%d" % i)) for i in range(2)])
    ab = Ring([(att[:, 1536 + i * 256:1536 + (i + 1) * 256].bitcast(BF16), Buf("ab%d" % i)) for i in range(2)])
    cb = Ring([(att[0:33, 2048 + i * 256:2048 + (i + 1) * 256].bitcast(BF16), Buf("cb%d" % i)) for i in range(2)])
    cfr = Ring([(att[0:33, 2560:3072], Buf("cfr0"))])
    units = attn_a_units()
    n = len(units)
    C = k.cbufs

    def s1(u):
        hh, g, j, c0 = u["hh"], u["g"], u["j"], u["c0"]
        u["P1"] = P1.next()
        bank, bb = u["P1"]
        kT = k_sb[:, hh, j * 128:(j + 1) * 128]
        qc = q_sb[:, hh, g * TG + c0:(g + 1) * TG]
        mm(k, bank[:, c0:TG], kT, qc, True, not u["diag"], kb[hh] + [qb[hh][g]], [bb], not u["diag"])
        if u["diag"]:
            mm(k, bank[:, c0:c0 + 128], k.ident, k.maskA, False, True, C, [bb], True)

    def actA1(u):
        c0 = u["c0"]
        bank, bb = u["P1"]
        u["eb"] = eb.next()
        e_, ebb = u["eb"]
        p.op("act", ("activation", dict(out=e_[:, c0:TG], in_=bank[:, c0:TG], func=AF.Exp)),
             reads=[bb], writes=[ebb])

    def actA2(u):
        c0 = u["c0"]
        e_, ebb = u["eb"]
        u["sp"] = spb.next()
        sp, spbb = u["sp"]
        p.op("act", ("activation", dict(out=sp[:, c0:TG], in_=e_[:, c0:TG], func=AF.Ln, bias=k.one, scale=1.0)),
             reads=[ebb] + C, writes=[spbb])

    def s2(u, nxt):
        hh, g, j, c0 = u["hh"], u["g"], u["j"], u["c0"]
        sp, spbb = u["sp"]
        if u["first"]:
            u["PO"] = PO.next()
            mm(k, u["PO"][0][:], k.ident, k.zeros[:], True, False, C, [u["PO"][1]], True)
            u["cf"] = cfr.next()
            p.op("dve", ("memset", dict(_pos=(u["cf"][0][:, :], 0.0))), writes=[u["cf"][1]])
            u["cb"] = None
        pc, pcb = PC.next()
        mm(k, pc[:, c0:TG], k.negones[:], sp[:, c0:TG], True, True, [spbb] + C, [pcb], True)
        if not u["last"]:
            nc0 = nxt["c0"]
            cft, cfb = u["cf"]
            p.op("dve", ("tensor_tensor", dict(out=cft[:, c0:TG], in0=cft[:, c0:TG], in1=pc[0:33, c0:TG],
                                               op=ALU.add)), reads=[pcb, cfb], writes=[cfb])
            cbt, cbb = cb.next()
            p.op("dve", ("tensor_copy", dict(out=cbt[:, nc0:TG], in_=cft[:, nc0:TG])),
                 reads=[cfb], writes=[cbb])
            p.op("dve", ("tensor_tensor", dict(out=cbt[32:33, nc0:TG], in0=cft[32:33, nc0:TG],
                                               in1=cbt[32:33, nc0:TG], op=ALU.subtract)),
                 reads=[cfb, cbb], writes=[cbb])
            nxt["cb"] = (cbt, cbb)
            nxt["PO"] = u["PO"]
            nxt["cf"] = u["cf"]
        u["P2"] = P2.next()
        bank, bb = u["P2"]
        kT = k_sb[:, hh, j * 128:(j + 1) * 128]
        qc = q_sb[:, hh, g * TG + c0:(g + 1) * TG]
        mm(k, bank[:, c0:TG], kT, qc, True, False, kb[hh] + [qb[hh][g]], [bb], False)
        has_c = u["cb"] is not None
        lastmm = not (has_c or u["diag"])
        mm(k, bank[:, c0:TG], k.negtri, sp[:, c0:TG], False, lastmm, [spbb] + C, [bb], lastmm)
        if has_c:
            cbt, cbb = u["cb"]
            lastmm = not u["diag"]
            mm(k, bank[:, c0:TG], k.sel33, cbt[:, c0:TG], False, lastmm, [cbb] + C, [bb], lastmm)
        if u["diag"]:
            mm(k, bank[:, c0:c0 + 128], k.ident, k.maskA, False, True, C, [bb], True)

    def actB(u):
        c0 = u["c0"]
        bank, bb = u["P2"]
        u["ab"] = ab.next()
        a_, abb = u["ab"]
        p.op("act", ("activation", dict(out=a_[:, c0:TG], in_=bank[:, c0:TG], func=AF.Exp)),
             reads=[bb], writes=[abb])

    def s3(u):
        hh, g, j, c0 = u["hh"], u["g"], u["j"], u["c0"]
        a_, abb = u["ab"]
        po, pob = u["PO"]
        vj = v_sb[:, hh, j * 128:(j + 1) * 128]
        mm(k, po[:, c0:TG], vj, a_[:, c0:TG], False, u["last"], vb[hh] + [abb], [pob], True)
        if u["last"]:
            p.op("dve", ("tensor_copy", dict(out=q_sb[:, hh, g * TG:(g + 1) * TG], in_=po[:])),
                 reads=[pob], writes=[qb[hh][g]])
            if on_group_done is not None:
                on_group_done(hh, g)
            if g == 7 and on_head_done is not None:
                on_head_done(hh)

    s1(units[0])
    for i in range(n + 1):
        if i + 1 < n:
            s1(units[i + 1])
        if i < n:
            actA1(units[i])
            actA2(units[i])
            s2(units[i], units[i + 1] if not units[i]["last"] else None)
        if i >= 1:
            actB(units[i - 1])
            s3(units[i - 1])


def attn_b(k, mem, q_sb, k_sb, v_sb, qb, kb, vb, on_head_done=None, on_group_done=None):
    p = k.p
    C = k.cbufs
    lam, lam2, sl, onesf = mem.lam, mem.lam2, mem.sl, mem.onesf
    lb, slb = mem.lam_b, mem.sl_b
    p.op("dve", ("memset", dict(_pos=(onesf[:], 1.0))), writes=[lb])
    p.op("dve", ("tensor_tensor", dict(out=lam2[:, 0:1], in0=lam[:, 0:1], in1=lam[:, 1:2], op=ALU.mult)),
         reads=[lb], writes=[lb])
    p.op("dve", ("tensor_tensor", dict(out=lam2[:, 1:2], in0=lam[:, 2:3], in1=lam[:, 3:4], op=ALU.mult)),
         reads=[lb], writes=[lb])
    bank, bb = k.banks.next()
    mm(k, bank[:, 0:2], onesf[:], lam2[:, 0:2], True, True, [lb], [bb], True)
    p.op("act", ("activation", dict(out=lam2[:, 2:4], in_=bank[:, 0:2], func=AF.Exp)), reads=[bb], writes=[lb])
    p.op("dve", ("scalar_tensor_tensor", dict(out=lam[:, 0:1], in0=lam2[:, 3:4], scalar=-LAMBDA_INIT,
                                                 in1=lam2[:, 2:3], op0=ALU.add, op1=ALU.subtract)),
         reads=[lb], writes=[lb])
    neglam = lam[:, 0:1]
    p.op("dve", ("tensor_scalar", dict(out=sl[:], in0=sl[:], scalar1=1.0 - LAMBDA_INIT, scalar2=None,
                                          op0=ALU.mult)), reads=[slb], writes=[slb])

    allb = k.banks.items
    P1 = Ring(allb[0:3])
    POS = Ring([[allb[3], allb[4]], [allb[5], allb[6]]])
    PL = allb[7]
    PX = P1
    att = mem.att
    pb = Ring([(att[:, o_:o_ + 256].bitcast(BF16), Buf("pb%d" % i)) for i, o_ in enumerate((0, 256, 2816))])
    on = [(att[:, 512 + i * 1024:512 + (i + 1) * 1024].rearrange("p (a t) -> p a t", a=2), Buf("on%d" % i))
          for i in range(2)]
    sqr = Ring([(att[:, 2560:2816].bitcast(BF16), Buf("sqr0"))])
    rl = mem.rstd.items[0]
    rsd = mem.rstd.items[1]

    for hh in range(2):
        for g in range(8):
            for c in range(2):
                hc = 2 * hh + c
                units = []
                for j in range(4 * g + 4):
                    c0 = max(0, (j - 4 * g) * 128)
                    units.append((j, c0, j >= 4 * g))
                PO = POS.next()
                for (bk, bkb) in (PO[0], PO[1]):
                    mm(k, bk[:], k.ident, k.zeros[:], True, False, C, [bkb], True)
                pq = []
                nu = len(units)
                st_ = {"plz": False}

                def av(item, PO=PO, st_=st_):
                    j, c0, pt, ptb, last = item
                    if not st_["plz"]:
                        mm(k, PL[0][:], k.ident, k.zeros[:], True, False, C, [PL[1]], True)
                        st_["plz"] = True
                    for half in range(2):
                        vj = v_sb[:, hh, j * 256 + half * 128: j * 256 + half * 128 + 128]
                        mm(k, PO[half][0][:, c0:TG], vj, pt[:, c0:TG], False, last, vb[hh] + [ptb],
                           [PO[half][1]], True)
                    mm(k, PL[0][:, c0:TG], k.ones, pt[:, c0:TG], False, last, [ptb] + C, [PL[1]], True)

                for ui in range(nu):
                    j, c0, diag = units[ui]
                    bank, bb = P1.next()
                    mm(k, bank[:, c0:TG], k_sb[:, hc, j * 128:(j + 1) * 128],
                       q_sb[:, hc, g * TG + c0:(g + 1) * TG], True, not diag, kb[hc] + [qb[hc][g]], [bb], not diag)
                    if diag:
                        mm(k, bank[:, c0:c0 + 128], k.ident, k.maskB, False, True, C, [bb], True)
                    pt, ptb = pb.next()
                    p.op("act", ("activation", dict(
                        out=pt[:, c0:TG], in_=bank[:, c0:TG], func=AF.Exp)), reads=[bb], writes=[ptb])
                    pq.append((j, c0, pt, ptb, ui == nu - 1))
                    if len(pq) > 2:
                        av(pq.pop(0))
                while pq:
                    av(pq.pop(0))
                p.op("act", ("activation", dict(out=rl[0][:], in_=PL[0][:], func=AF.Ln)), reads=[PL[1]], writes=[rl[1]])
                p.op("act", ("activation", dict(out=rl[0][:], in_=rl[0][:], func=AF.Exp, scale=-1.0)),
                     reads=[rl[1]], writes=[rl[1]])
                for half in range(2):
                    p.op("dve", ("tensor_tensor", dict(
                        out=on[c][0][:, half, :], in0=PO[half][0][:], in1=rl[0][:], op=ALU.mult)),
                         reads=[PO[half][1], rl[1]], writes=[on[c][1]])
            o_, o_b = on[0]
            p.op("dve", ("scalar_tensor_tensor", dict(out=o_[:], in0=on[1][0][:], scalar=neglam, in1=o_[:],
                                                         op0=ALU.mult, op1=ALU.add)),
                 reads=[on[1][1], o_b, lb], writes=[o_b])
            bx, bxb = PX.next()
            for half in range(2):
                sq, sqb = sqr.next()
                p.op("act", ("activation", dict(out=sq[:], in_=o_[:, half, :],
                                                                     func=AF.Square)), reads=[o_b], writes=[sqb])
                mm(k, bx[:], k.ones, sq[:], half == 0, half == 1, [sqb] + C, [bxb], True)
            p.op("act", ("activation", dict(out=rsd[0][:], in_=bx[:], func=AF.Ln, bias=k.eps,
                                                      scale=1.0 / 256)), reads=[bxb] + C, writes=[rsd[1]])
            p.op("act", ("activation", dict(out=rsd[0][:], in_=rsd[0][:], func=AF.Exp, scale=-0.5)),
                 reads=[rsd[1]], writes=[rsd[1]])
            for half in range(2):
                p.op("dve", ("scalar_tensor_tensor", dict(
                    out=q_sb[:, 2 * hh + half, g * TG:(g + 1) * TG], in0=o_[:, half, :],
                    scalar=sl[:, half:half + 1], in1=rsd[0][:], op0=ALU.mult, op1=ALU.mult)),
                     reads=[o_b, rsd[1], slb], writes=[qb[2 * hh + half][g]])
            if on_group_done is not None:
                on_group_done(hh, g)
            if on_head_done is not None and g in (3, 7):
                on_head_done(hh, g // 4)


def build_fused():
    nc = bass.Bass("TRN2", target_bir_lowering=False)
    GROUPS = [[0, 1, 2, 3], [4, 5, 6, 7]]
    with ExitStack() as stack:
        k = K(nc, stack)
        p = k.p
        x_d = k.din("xT", [D, T])
        cm_d = k.din("cm", [128, 768], BF16)
        g_d = k.din("gains", [128, 8 * KC])
        gn_d = k.din("gn", [128, 4])
        idx_d = k.din("idx", [128, 4], mybir.dt.uint32)
        cos_d = k.din("cosT", [128, T])
        sin_d = k.din("sinT", [128, T])
        lam_d = k.din("lamv", [128, 4])
        sl_d = k.din("subln", [128, 2])
        perm_d = k.din("perm", [128, 128])
        wqk_d = k.din("wqk", [16, 128, 4096])
        wva_d = k.din("wv_a", [8, 128, 4096])
        woa_d = k.din("wo_a", [8, 128, 4096])
        wgua_d = k.din("wgu_a", [FC, 128, 4096])
        wda_d = k.din("wd_a", [NQ, 8, 128, 2048])
        wk_d = k.din("wk_s", [8, 128, 4096])
        wvs_d = k.din("wv_s", [8, 128, 4096])
        wq_d = k.din("wq_b", [8, 128, 4096])
        wob_d = k.din("wo_b", [8, 128, 4096])
        wgub_d = k.din("wgu_b", [FC, 128, 4096])
        wdb_d = k.din("wd_b", [NQ, 8, 128, 2048])
        x_o = k.dout("xo", [D, T])
        exA_in = nc.dram_tensor("exA_in", [12 * 512, 1024], BF16).ap()
        exA_out = nc.dram_tensor("exA_out", [12 * 2048, 1024], BF16).ap()
        exO_in = nc.dram_tensor("exO_in", [4 * 512, 1024], BF16).ap()
        exO_out = nc.dram_tensor("exO_out", [4 * 2048, 1024], BF16).ap()
        warm_in = nc.dram_tensor("warm_in", [16, 1024], BF16).ap()
        warm_out = nc.dram_tensor("warm_out", [64, 1024], BF16).ap()
        k.consts(cm_d)
        mem = Mem(k)
        mem.lam = k.sb([128, 4], F32, "lam")
        mem.lam2 = k.sb([128, 4], F32, "lam2")
        mem.sl = k.sb([128, 2], F32, "sl")
        mem.onesf = k.sb([128, 128], F32, "onesf")
        mem.perm = k.sb([128, 128], F32, "perm")
        mem.perm_b = Buf("perm")
        mem.lam_b = Buf("lam")
        mem.sl_b = Buf("sl")
        p.dma("sp", mem.gains[:], g_d, writes=[mem.gains_b])
        p.dma("sp", mem.gn[:], gn_d, writes=[mem.gn_b])
        p.dma("sp", mem.idx[:], idx_d, writes=[mem.idx_b])
        p.dma("sp", mem.lam[:], lam_d, writes=[mem.lam_b])
        p.dma("sp", mem.sl[:], sl_d, writes=[mem.sl_b])
        p.dma("sp", mem.perm[:], perm_d, writes=[mem.perm_b])
        idx0 = mem.idx[:, 0:1]
        idx1 = mem.idx[:, 1:2]
        idx2 = mem.idx[:, 2:3]

        def piece_ag(in_ap, out_ap, pc, dma_bufs, name):
            b = Buf(name)
            p.allgather(in_ap[pc * 512:(pc + 1) * 512, :], out_ap[pc * 2048:(pc + 1) * 2048, :], GROUPS,
                        reads=dma_bufs, writes=[b])
            return b

        def keep_warm(hh, g):
            if g != 7:
                p.allgather(warm_in, warm_out, GROUPS)

        cur = {}

        st = TokState(k, mem)
        load_x(k, st, x_d)
        rmsnorm(k, st, 0)
        hfn = lambda kc, tg: st.h[:, kc, tg * TG:(tg + 1) * TG]
        hbn = lambda kc, tg: [st.hb[tg]]
        ccA = {}

        late = []

        def sched(cc, pc, bufs, st_, name, early=True):
            fn = lambda: cc.__setitem__(pc, piece_ag(exA_in, exA_out, pc, bufs, name))
            if early:
                st_.defer(fn)
            else:
                late.append(fn)

        def run_late():
            for fn in late:
                fn()
            del late[:]

        for hh in range(4):
            pb_ = {0: [], 1: []}

            def ev(m, tg, bank, bb, hh=hh, pb_=pb_):
                t, j = m // 4, m % 4
                ts = slice(tg * TG, (tg + 1) * TG)
                if tg == 0:
                    cur["s"] = st.stg.next()
                s_, sb_ = cur["s"]
                evac_scaled(k, st, s_[:, ts], bank, bb, SCALE if t == 0 else 1.0, [sb_])
                if tg == 1:
                    pc = hh * 3 + t
                    r0 = pc * 512 + j * 128
                    db = Buf("dA")
                    p.dma("sp", exA_in[r0:r0 + 128, :], s_[:, 0:T], reads=[sb_], writes=[db])
                    pb_[t].append(db)
                    if j == 3:
                        sched(ccA, pc, list(pb_[t]), st, "ccA%d" % pc, early=(hh == 0))

            linear_fm(k, st, wqk_d[4 * hh:4 * hh + 4], 4, 2, KC, hfn, hbn, ev)
            vbufs = []

            def out_v_a(l, vacc, vb_, hh=hh, vbufs=vbufs):
                pc = hh * 3 + 2
                for e_ in range(2):
                    j = 2 * l + e_
                    r0 = pc * 512 + j * 128
                    db = Buf("dV")
                    p.dma("sp", exA_in[r0:r0 + 128, :].rearrange("p (t d) -> p t d", d=128),
                          vacc[:, :, e_ * 128:(e_ + 1) * 128], reads=[vb_], writes=[db])
                    vbufs.append(db)
                if l == 1:
                    sched(ccA, pc, list(vbufs), st, "ccA%d" % pc, early=(hh == 0))

            linear_tm_v(k, st, wva_d[2 * hh:2 * hh + 2], 2, 256, out_v_a)
        st.flush()
        p.barrier()

        q_sb = mem.v3(mem.A1, 4)
        k_sb = mem.v3(mem.A2, 4)
        v_sb = mem.A3[:, :].bitcast(BF16).rearrange("p (k t) -> p k t", k=4)
        qb = [[Buf("q%d_%d" % (h, g)) for g in range(8)] for h in range(4)]
        kb = [[Buf("k%d_%d" % (h, r)) for r in range(4)] for h in range(4)]
        vb = [[Buf("v%d_%d" % (h, r)) for r in range(4)] for h in range(4)]
        for hh in range(4):
            if hh == 1:
                run_late()
            for r in range(4):
                rs = slice(r * T, (r + 1) * T)
                for t, (dst, wb_) in enumerate(((q_sb, [qb[hh][2 * r], qb[hh][2 * r + 1]]), (k_sb, [kb[hh][r]]),
                                                (v_sb, [vb[hh][r]]))):
                    pc = hh * 3 + t
                    p.gather(dst[:, hh, rs], exA_out, idx0, (pc * 2048 + r * 512) * 1024,
                             reads=[ccA[pc], mem.idx_b], writes=wb_)
        def prefetch(w_dram, nl):
            out = []
            for l in range(nl):
                slot, b = mem.wslots.next()
                p.dma("pool", slot[:, 0:4096], w_dram[l], writes=[b])
                out.append((slot, b))
            return out

        pre_wo = prefetch(woa_d, 3)
        ccO = {}

        def o_done(hc):
            dbs = []
            for r in range(4):
                r0 = (hc * 4 + r) * 128
                db = Buf("dO")
                p.dma("sp", exO_in[r0:r0 + 128, :], q_sb[:, hc, r * T:(r + 1) * T],
                      reads=[qb[hc][2 * r], qb[hc][2 * r + 1]], writes=[db])
                dbs.append(db)
            ccO[hc] = piece_ag(exO_in, exO_out, hc, dbs, "ccO%d" % hc)

        attn_a(k, mem, q_sb, k_sb, v_sb, qb, kb, vb, on_head_done=o_done, on_group_done=keep_warm)
        p.barrier()

        def gather_o(st_):
            for hc in range(4):
                for j in range(4):
                    p.gather(st_.qo[:, 4 * j + hc, :], exO_out, idx0, (hc * 2048 + j * 512) * 1024,
                             reads=[ccO[hc], mem.idx_b], writes=st_.qob[4 * j + hc])

        st = TokState(k, mem)
        gather_o(st)
        hfn = lambda kc, tg: st.h[:, kc, tg * TG:(tg + 1) * TG]
        hbn = lambda kc, tg: [st.hb[tg]]
        qof = lambda kc, tg: st.qo[:, kc, tg * TG:(tg + 1) * TG]
        qobf = lambda kc, tg: [st.qob[kc][tg]]
        resid = lambda m, tg, bank, bb: evac_resid(k, st, m, tg, bank, bb)
        linear_fm(k, st, woa_d, 8, 2, KC, qof, qobf, resid, pre=pre_wo)
        ffn(k, st, wgua_d, wda_d, 16)
        rp = RopeState(k, mem)
        rope_tables(k, rp, cos_d, sin_d, 0)
        rmsnorm(k, st, 32, keep=True)
        ccB = {}

        def mk_ev_rope(hh, pc0, scale, st_):
            pb_ = {0: [], 1: []}

            def ev(m, tg, bank, bb):
                c, j = m // 4, m % 4
                ts = slice(tg * TG, (tg + 1) * TG)
                if tg == 0:
                    cur["s"] = st_.stg.next()
                s_, sb_ = cur["s"]
                cont = qknorm_rope_chunk(k, st_, rp, bank, bb, tg, s_[:, ts], [sb_], scale)

                def tail(cont=cont, s_=s_, sb_=sb_, tg=tg, c=c, j=j):
                    cont()
                    if tg == 1:
                        pc = hh * 6 + pc0 + c
                        r0 = pc * 512 + j * 128
                        db = Buf("dB")
                        p.dma("sp", exA_in[r0:r0 + 128, :], s_[:, 0:T], reads=[sb_], writes=[db])
                        pb_[c].append(db)
                        if j == 3:
                            sched(ccB, pc, list(pb_[c]), st_, "ccB%d" % pc, early=(hh == 0))
                return tail
            return ev

        for hh in range(2):
            linear_fm(k, st, wk_d[4 * hh:4 * hh + 4], 4, 2, KC, hfn, hbn, mk_ev_rope(hh, 2, 1.0, st))
            vbufs = []

            def out_v_s(l, vacc, vb_, hh=hh, vbufs=vbufs):
                pc = hh * 6 + 4 + l // 2
                r0 = pc * 512 + (l % 2) * 256
                dst = exA_in[r0:r0 + 256, :].rearrange("(p s) e -> p (s e)", s=2).rearrange("p (t c) -> p t c", c=256)
                db = Buf("dVs")
                p.dma("sp", dst, vacc, reads=[vb_], writes=[db])
                vbufs.append(db)
                if l % 2 == 1:
                    sched(ccB, pc, list(vbufs[-2:]), st, "ccB%d" % pc, early=(hh == 0))

            linear_tm_v(k, st, wvs_d[4 * hh:4 * hh + 4], 4, 256, out_v_s)
        rope_tables(k, rp, cos_d, sin_d, 2)
        rmsnorm(k, st, 48, reuse=True)
        for hh in range(2):
            linear_fm(k, st, wq_d[4 * hh:4 * hh + 4], 4, 2, KC, hfn, hbn, mk_ev_rope(hh, 0, SCALE, st))
        st.flush()
        p.barrier()

        v_sb2 = mem.A3[:, :].bitcast(BF16).rearrange("p (k t) -> p k t", k=2)
        qb = [[Buf("qB%d_%d" % (h, g)) for g in range(8)] for h in range(4)]
        kb = [[Buf("kB%d_%d" % (h, r)) for r in range(4)] for h in range(4)]
        vb = [[Buf("vB%d_%d" % (h, i)) for i in range(8)] for h in range(2)]
        for hh in range(2):
            if hh == 1:
                run_late()
            for c in range(2):
                hc = 2 * hh + c
                for r in range(4):
                    rs = slice(r * T, (r + 1) * T)
                    pcq, pck = hh * 6 + c, hh * 6 + 2 + c
                    p.gather(q_sb[:, hc, rs], exA_out, idx0, (pcq * 2048 + r * 512) * 1024,
                             reads=[ccB[pcq], mem.idx_b], writes=[qb[hc][2 * r], qb[hc][2 * r + 1]])
                    p.gather(k_sb[:, hc, rs], exA_out, idx0, (pck * 2048 + r * 512) * 1024,
                             reads=[ccB[pck], mem.idx_b], writes=[kb[hc][r]])
            for r in range(4):
                for s_ in range(2):
                    p.gather(v_sb2[:, hh, r * 2048 + s_ * 1024:r * 2048 + (s_ + 1) * 1024], exA_out, idx1,
                             ((hh * 6 + 4) * 2048 + r * 512 + s_) * 1024,
                             reads=[ccB[hh * 6 + 4], ccB[hh * 6 + 5], mem.idx_b], writes=[vb[hh][2 * r + s_]])
        pre_wo = prefetch(wob_d, 3)
        ccO = {}

        def o_done_b(hh, a):
            pc = hh * 2 + a
            dbs = []
            for rr in range(2):
                r = 2 * a + rr
                for half in range(2):
                    hc = 2 * hh + half
                    r0 = pc * 512 + (rr * 2 + half) * 128
                    db = Buf("dOb")
                    p.dma("sp", exO_in[r0:r0 + 128, :], q_sb[:, hc, r * T:(r + 1) * T],
                          reads=[qb[hc][2 * r], qb[hc][2 * r + 1]], writes=[db])
                    dbs.append(db)
            ccO[pc] = piece_ag(exO_in, exO_out, pc, dbs, "ccOb%d" % pc)

        def gather_o_b(st_):
            for hh in range(2):
                for half in range(2):
                    for j in range(4):
                        p.gather(st_.qo[:, 4 * j + 2 * hh + half, :], exO_out, idx2,
                                 (hh * 2 * 2048 + j * 512 + half * 128) * 1024,
                                 reads=[ccO[hh * 2], ccO[hh * 2 + 1], mem.idx_b], writes=st_.qob[4 * j + 2 * hh + half])

        attn_b(k, mem, q_sb, k_sb, v_sb2, qb, kb, vb, on_head_done=o_done_b, on_group_done=keep_warm)
        p.barrier()

        st = TokState(k, mem)
        gather_o_b(st)
        qof = lambda kc, tg: st.qo[:, kc, tg * TG:(tg + 1) * TG]
        qobf = lambda kc, tg: [st.qob[kc][tg]]
        resid = lambda m, tg, bank, bb: evac_resid(k, st, m, tg, bank, bb)
        linear_fm(k, st, wob_d, 8, 2, KC, qof, qobf, resid, pre=pre_wo)
        ffn(k, st, wgub_d, wdb_d, 64)
        store_x(k, st, x_o)
        p.emit()
    return nc


def w_fm(w, cpl):
    kd, n = w.shape
    kc = kd // 128
    a = w.reshape(kc, 128, n // 128, 128)
    a = a.transpose(2, 1, 0, 3)
    a = a.reshape(n // 128 // cpl, cpl, 128, kc, 128).transpose(0, 2, 1, 3, 4)
    return np.ascontiguousarray(a.reshape(n // 128 // cpl, 128, cpl * kc * 128))


def w_tm(w, ncol):
    kd, n = w.shape
    kc = kd // 128
    a = w.reshape(kc, 128, n // ncol, ncol).transpose(2, 1, 0, 3)
    return np.ascontiguousarray(a.reshape(n // ncol, 128, kc * ncol))


def w_gu(w):
    g = w_fm(w[:, :FF], 1)
    u = w_fm(w[:, FF:], 1)
    return np.ascontiguousarray(np.concatenate([g, u], axis=2))


def w_down(w):
    out = np.zeros((NQ, 8, 128, 2, 8, 128), np.float32)
    f0 = 0
    for q in range(NQ):
        fq = FQS[q]
        a = w_fm(w[f0 * 128:(f0 + fq) * 128, :], 2).reshape(8, 128, 2, fq, 128)
        out[q][:, :, :, :fq, :] = a
        f0 += fq
    return out.reshape(NQ, 8, 128, 2048)


def gain_fm(g):
    return np.ascontiguousarray(g.reshape(KC, 128).T)


def const_mats():
    i = np.arange(128)
    ident = np.eye(128, dtype=np.float32)
    negtri = -(i[:, None] >= i[None, :]).astype(np.float32)
    sel = np.zeros((128, 128), np.float32)
    sel[0, :] = 1.0
    sel[32, :] = 1.0
    maskA = np.where(i[:, None] >= i[None, :], NEG, 0.0).astype(np.float32)
    maskB = np.where(i[:, None] > i[None, :], NEG, 0.0).astype(np.float32)
    ones = np.ones((128, 128), np.float32)
    return np.concatenate([ident, negtri, sel, maskA, maskB, ones], axis=1).astype(BF)


def rope_tabs(pos):
    inv = ROPE_THETA ** (-np.arange(0, 128, 2, dtype=np.float32) / np.float32(128))
    ang = pos.astype(np.float32)[:, None] * inv.astype(np.float32)[None, :]
    ang = np.concatenate([ang, ang], axis=-1).astype(np.float32)
    cos = np.cos(ang).astype(np.float32).T
    sin = np.sin(ang).astype(np.float32).T.copy()
    sin[:64] *= -1.0
    return np.ascontiguousarray(cos), np.ascontiguousarray(sin)


_CACHE = {}


def kernel(x, a_attn_norm, a_w_qkv, a_w_o, a_ffn_norm, a_w_gate_up, a_w_down, kv_norm, w_kv, k_norm,
           b_attn_norm, b_w_q, b_q_norm, b_lambda_q1, b_lambda_k1, b_lambda_q2, b_lambda_k2,
           b_subln, b_w_o, b_ffn_norm, b_w_gate_up, b_w_down):
    f = lambda a: np.asarray(a, dtype=np.float32)
    x = f(x)
    gains = np.zeros((128, 8 * KC), np.float32)
    for i, g in enumerate([a_attn_norm[0], a_ffn_norm[0], kv_norm, b_attn_norm[0], b_ffn_norm[0]]):
        gains[:, i * KC:(i + 1) * KC] = gain_fm(f(g))
    kn = f(k_norm)
    qn = f(b_q_norm[0])
    gn = np.stack([kn, np.roll(kn, 64), qn, np.roll(qn, 64)], axis=1).astype(np.float32)
    lamv = np.ascontiguousarray(np.stack([f(v).reshape(128) for v in
                                          (b_lambda_q1[0], b_lambda_k1[0], b_lambda_q2[0], b_lambda_k2[0])], axis=1))
    subln = np.ascontiguousarray(f(b_subln[0]).reshape(2, 128).T)
    wqkv = f(a_w_qkv[0])
    wkv = f(w_kv)
    c128 = np.arange(128)
    cols_qk_a = np.concatenate([t * D + (4 * j + hh) * 128 + c128 for hh in range(4) for t in range(2) for j in range(4)])
    cols_v_a = np.concatenate([2 * D + (4 * j + hh) * 128 + c128 for hh in range(4) for j in range(4)])
    cols_k_s = np.concatenate([(4 * j + 2 * hh + c) * 128 + c128 for hh in range(2) for c in range(2) for j in range(4)])
    cols_v_s = np.concatenate([D + (2 * j + hh) * 256 + np.arange(256) for hh in range(2) for j in range(4)])
    common = dict(
        cm=const_mats(), perm=np.ascontiguousarray(np.roll(np.eye(128, dtype=np.float32), 64, axis=0)), gains=gains, gn=gn, lamv=lamv, subln=subln,
        wqk=w_fm(wqkv[:, cols_qk_a], 2), wv_a=w_tm(wqkv[:, cols_v_a], 256),
        wo_a=w_fm(f(a_w_o[0]), 2), wgu_a=w_gu(f(a_w_gate_up[0])), wd_a=w_down(f(a_w_down[0])),
        wk_s=w_fm(wkv[:, cols_k_s], 2), wv_s=w_tm(wkv[:, cols_v_s], 256), wq_b=w_fm(f(b_w_q[0])[:, cols_k_s], 2),
        wo_b=w_fm(f(b_w_o[0]), 2), wgu_b=w_gu(f(b_w_gate_up[0])), wd_b=w_down(f(b_w_down[0])),
    )
    cores = [(b, r) for b in range(2) for r in range(4)]
    tabs = [rope_tabs(np.arange(r * T, (r + 1) * T)) for r in range(4)]
    pidx = np.arange(128, dtype=np.uint32)
    in_maps = []
    for (b, r) in cores:
        d = dict(common)
        d["xT"] = np.ascontiguousarray(x[b, r * T:(r + 1) * T, :].T)
        d["cosT"], d["sinT"] = tabs[r]
        d["idx"] = np.ascontiguousarray(np.stack([r * 128 + pidx, (r // 2) * 2048 + (r % 2) * 256 + 2 * pidx,
                                                  (r // 2) * 2048 + (r % 2) * 256 + pidx, 0 * pidx],
                                                 axis=1).astype(np.uint32))
        in_maps.append(d)
    if "nc" not in _CACHE:
        _CACHE["nc"] = build_fused()
    res = run_bass_kernel_spmd(_CACHE["nc"], in_maps, core_ids=list(range(8))).results
    out = np.empty((2, S, D), np.float32)
    for i, (b, r) in enumerate(cores):
        out[b, r * T:(r + 1) * T, :] = np.asarray(res[i]["xo"]).T
    return out
```

```python
import math
from contextlib import ExitStack

import numpy as np
import ml_dtypes

import concourse.bass as bass
import concourse.mybir as mybir
from concourse.bass_utils import run_bass_kernel_spmd

F32 = mybir.dt.float32
BF16 = mybir.dt.bfloat16
AF = mybir.ActivationFunctionType
ALU = mybir.AluOpType
BF = ml_dtypes.bfloat16

D = 2048
KC = 16
T = 1024
S = 4096
NTG = 2
TG = 512
FF = 5632
FC = 44
FQS = [8, 8, 7, 7, 7, 7]
NQ = len(FQS)
EPS = 1e-6
NEG = -30000.0
ROPE_THETA = 10000.0
LAMBDA_INIT = 0.8 - 0.6 * math.exp(-0.3 * 1)
SCALE = 128.0 ** -0.5
NDS = 12


class Buf:
    __slots__ = ("name", "w", "r")

    def __init__(self, name):
        self.name = name
        self.w = None
        self.r = []


class Prog:
    ENG = ("pe", "act", "dve", "pool", "sp")

    def __init__(self, nc, stack):
        self.nc = nc
        self.q = {e: [] for e in self.ENG}
        self.sem = {e: stack.enter_context(nc.semaphore("pg_" + e)) for e in self.ENG}
        self.cnt = dict.fromkeys(self.ENG, 0)
        self.waited = {e: {} for e in self.ENG}
        self.pend_r = {e: [] for e in self.ENG}
        self.pend_w = {e: [] for e in self.ENG}
        self.dsem = {e: [stack.enter_context(nc.semaphore("d_%s%d" % (e, i))) for i in range(NDS)]
                     for e in ("sp", "pool")}
        self.dcnt = {e: [0] * NDS for e in ("sp", "pool")}
        self.dnext = {"sp": 0, "pool": 0}
        self.out_toks = []
        self.ccsem = stack.enter_context(nc.semaphore("cc_sem"))
        self.ccnt = 0

    def _wait(self, eng, tok):
        sem, val, src = tok
        if src == "pe" and eng == "pe":
            return
        key = id(sem)
        if self.waited[eng].get(key, 0) >= val:
            return
        self.waited[eng][key] = val
        self.q[eng].append(lambda e, sem=sem, val=val: e.wait_ge(sem, val))

    def _deps(self, eng, reads, writes):
        for b in reads:
            if b.w is not None:
                self._wait(eng, b.w)
        for b in writes:
            if b.w is not None:
                self._wait(eng, b.w)
            for t in b.r:
                self._wait(eng, t)

    def op(self, eng, call, reads=(), writes=(), mark=True):
        name, kw = call
        pos = kw.pop("_pos", ())
        fn = (lambda e, name=name, kw=kw, pos=pos: getattr(e, name)(*pos, **kw))
        self._deps(eng, reads, writes)
        if not mark:
            self.q[eng].append(lambda e, fn=fn: fn(e))
            self.pend_r[eng].extend(reads)
            self.pend_w[eng].extend(writes)
            return None
        self.cnt[eng] += 1
        sem = self.sem[eng]
        tok = (sem, self.cnt[eng], eng)
        self.q[eng].append(lambda e, fn=fn, sem=sem: fn(e).then_inc(sem, 1))
        for b in self.pend_r[eng]:
            b.r.append(tok)
        for b in self.pend_w[eng]:
            b.w = tok
            b.r = []
        self.pend_r[eng] = []
        self.pend_w[eng] = []
        for b in reads:
            b.r.append(tok)
        for b in writes:
            b.w = tok
            b.r = []
        return tok

    def dma(self, eng, out, in_, reads=(), writes=(), is_out=False):
        i = self.dnext[eng]
        self.dnext[eng] = (i + 1) % NDS
        sem = self.dsem[eng][i]
        prev = self.dcnt[eng][i]
        if prev > 0:
            self._wait(eng, (sem, prev, "dma"))
        self._deps(eng, reads, writes)
        self.dcnt[eng][i] = prev + 16
        tok = (sem, prev + 16, "dma")
        self.q[eng].append(lambda e, out=out, in_=in_, sem=sem: e.dma_start(out=out, in_=in_).then_inc(sem, 16))
        for b in reads:
            b.r.append(tok)
        for b in writes:
            b.w = tok
            b.r = []
        if is_out:
            self.out_toks.append(tok)
        return tok

    def gather(self, out, in_, idx_ap, elem_off, reads=(), writes=()):
        eng = "pool"
        i = self.dnext[eng]
        self.dnext[eng] = (i + 1) % NDS
        sem = self.dsem[eng][i]
        prev = self.dcnt[eng][i]
        if prev > 0:
            self._wait(eng, (sem, prev, "dma"))
        self._deps(eng, reads, writes)
        self.dcnt[eng][i] = prev + 16
        tok = (sem, prev + 16, "dma")
        self.q[eng].append(lambda e, out=out, in_=in_, idx_ap=idx_ap, elem_off=elem_off, sem=sem: e.indirect_dma_start(
            out=out, out_offset=None, in_=in_, in_offset=bass.IndirectOffsetOnAxis(ap=idx_ap, axis=0),
            element_offset=elem_off).then_inc(sem, 16))
        for b in reads:
            b.r.append(tok)
        for b in writes:
            b.w = tok
            b.r = []
        return tok

    def allgather(self, in_ap, out_ap, groups, reads=(), writes=()):
        eng = "pool"
        self._deps(eng, reads, writes)
        self.ccnt += 1
        sem = self.ccsem
        tok = (sem, self.ccnt, "dma")
        self.q[eng].append(lambda e, in_ap=in_ap, out_ap=out_ap, sem=sem: e.collective_compute(
            "AllGather", ALU.bypass, replica_groups=groups, ins=[in_ap], outs=[out_ap]).then_inc(sem))
        for b in reads:
            b.r.append(tok)
        for b in writes:
            b.w = tok
            b.r = []
        return tok

    def barrier(self):
        for e in self.ENG:
            assert not self.pend_r[e] and not self.pend_w[e], "barrier inside an open PE group"
        for e in self.ENG:
            for e2 in self.ENG:
                if e2 != e and self.cnt[e2] > 0:
                    self._wait(e, (self.sem[e2], self.cnt[e2], e2))
            for qn in ("sp", "pool"):
                for i in range(NDS):
                    if self.dcnt[qn][i] > 0:
                        self._wait(e, (self.dsem[qn][i], self.dcnt[qn][i], "dma"))

    def emit(self):
        for tok in self.out_toks:
            self._wait("sp", tok)
        for e in ("pe", "act", "dve", "pool"):
            if self.cnt[e] > 0:
                self._wait("sp", (self.sem[e], self.cnt[e], e))
        for e in ("sp", "pool"):
            for i in range(NDS):
                if self.dcnt[e][i] > 0:
                    self._wait("sp", (self.dsem[e][i], self.dcnt[e][i], "dma"))
        q = self.q
        with self.nc.Block() as block:
            @block.tensor
            def _(e):
                for f in q["pe"]:
                    f(e)

            @block.scalar
            def _(e):
                for f in q["act"]:
                    f(e)

            @block.vector
            def _(e):
                for f in q["dve"]:
                    f(e)

            @block.gpsimd
            def _(e):
                for f in q["pool"]:
                    f(e)

            @block.sync
            def _(e):
                for f in q["sp"]:
                    f(e)


class Ring:
    def __init__(self, items):
        self.items = items
        self.i = 0

    def next(self):
        it = self.items[self.i]
        self.i = (self.i + 1) % len(self.items)
        return it


class K:
    def __init__(self, nc, stack):
        self.nc = nc
        self.stack = stack
        self.p = Prog(nc, stack)
        self.nalloc = 0
        self.banks = Ring([(self.psum("bank%d" % i), Buf("bank%d" % i)) for i in range(8)])

    def sb(self, shape, dtype, name=None):
        self.nalloc += 1
        t = self.stack.enter_context(self.nc.sbuf_tensor("sb_" + (name or ("t%d" % self.nalloc)), list(shape), dtype))
        return t

    def psum(self, name):
        return self.stack.enter_context(self.nc.psum_tensor("ps_" + name, [128, 512], F32))

    def din(self, name, shape, dtype=F32):
        return self.nc.dram_tensor(name, list(shape), dtype, kind="ExternalInput").ap()

    def dout(self, name, shape, dtype=F32):
        return self.nc.dram_tensor(name, list(shape), dtype, kind="ExternalOutput").ap()

    def consts(self, cm_ap):
        p = self.p
        self.cm = self.sb([128, 6 * 128], BF16, "cm")
        self.cm_b = Buf("cm")
        p.dma("sp", self.cm[:], cm_ap, writes=[self.cm_b])
        self.ident = self.cm[:, 0:128]
        self.negtri = self.cm[:, 128:256]
        self.sel33 = self.cm[0:33, 256:384]
        self.maskA = self.cm[:, 384:512]
        self.maskB = self.cm[:, 512:640]
        self.ones = self.cm[:, 640:768]
        self.cf = self.sb([128, 4], F32, "cf")
        self.cf_b = Buf("cf")
        p.op("dve", ("memset", dict(_pos=(self.cf[:, 0:1], EPS))), writes=[self.cf_b])
        p.op("dve", ("memset", dict(_pos=(self.cf[:, 1:2], 1.0))), writes=[self.cf_b])
        p.op("dve", ("memset", dict(_pos=(self.cf[:, 2:3], 0.0))), writes=[self.cf_b])
        self.eps = self.cf[:, 0:1]
        self.one = self.cf[:, 1:2]
        self.zero = self.cf[:, 2:3]
        self.negones = self.sb([128, 128], BF16, "negones")
        self.zeros = self.sb([128, 512], BF16, "zeros")
        self.cz_b = Buf("cz")
        p.op("dve", ("memset", dict(_pos=(self.negones[:], -1.0))), writes=[self.cz_b])
        p.op("dve", ("memset", dict(_pos=(self.zeros[:], 0.0))), writes=[self.cz_b])
        self.cbufs = [self.cm_b, self.cf_b, self.cz_b]


def mm(k, out, lhsT, rhs, start, stop, reads, writes, mark):
    return k.p.op("pe", ("matmul", dict(_pos=(out,), lhsT=lhsT, rhs=rhs, start=start, stop=stop)),
                  reads=reads, writes=writes, mark=mark)


class Mem:
    def __init__(self, k):
        self.k = k
        self.x = k.sb([128, KC, T], F32, "xT")
        self.xb = [[Buf("x%d_%d" % (m, tg)) for tg in range(NTG)] for m in range(KC)]
        self.A1 = k.sb([128, KC * T], BF16, "A1")
        self.A2 = k.sb([128, KC * T], BF16, "A2")
        self.A3 = k.sb([128, 8192], F32, "A3")
        self.att = k.sb([128, 3072], F32, "att")
        self.wslots = Ring([(k.sb([128, 4096], BF16, "w%d" % i), Buf("w%d" % i)) for i in range(3)])
        self.sq = Ring([(k.sb([128, TG], BF16, "sq%d" % i), Buf("sq%d" % i)) for i in range(2)])
        self.rstd = Ring([(k.sb([128, TG], F32, "rstd%d" % i), Buf("rstd%d" % i)) for i in range(2)])
        self.gains = k.sb([128, 8 * KC], F32, "gains")
        self.gains_b = Buf("gains")
        self.gn = k.sb([128, 4], F32, "gn")
        self.gn_b = Buf("gn")
        self.idx = k.sb([128, 4], mybir.dt.uint32, "idx")
        self.idx_b = Buf("idx")

    def v3(self, ap, n):
        return ap[:, :].rearrange("p (k t) -> p k t", k=n)


class TokState:
    def __init__(self, k, mem):
        self.k = k
        self.mem = mem
        self.x = mem.x
        self.xb = mem.xb
        self.h = mem.v3(mem.A1, KC)
        self.hb = [Buf("h%d" % tg) for tg in range(NTG)]
        self.qo = mem.v3(mem.A2, KC)
        self.qob = [[Buf("qo%d_%d" % (m, tg)) for tg in range(NTG)] for m in range(KC)]
        self.wslots = mem.wslots
        self.sq = mem.sq
        self.rstd = mem.rstd
        A3 = mem.A3
        self.tmpf = Ring([(A3[:, i * 512:(i + 1) * 512], Buf("tmpf%d" % i)) for i in range(4)])
        self.stg = Ring([(A3[:, 2048 + i * 1024:2048 + (i + 1) * 1024].bitcast(BF16), Buf("stg%d" % i))
                         for i in range(2)])
        self.gains = mem.gains
        self.gains_b = mem.gains_b
        self.keep_rstd = [(A3[:, 7168 + i * 512:7168 + (i + 1) * 512], Buf("krstd%d" % i)) for i in range(2)]
        self.evac_i = 0
        self.deferred = []

    def wload(self, dram_ap, n):
        k = self.k
        slot, b = self.wslots.next()
        k.p.dma("pool", slot[:, 0:n], dram_ap, writes=[b])
        self.tick()
        return slot, b

    def defer(self, fn, delay=2):
        self.deferred.append([delay, fn])

    def tick(self):
        keep = []
        for it in self.deferred:
            it[0] -= 1
            if it[0] <= 0:
                it[1]()
            else:
                keep.append(it)
        self.deferred = keep

    def flush(self):
        for it in self.deferred:
            it[1]()
        self.deferred = []


def load_x(k, st, x_dram):
    for kc in range(KC):
        k.p.dma("sp", st.x[:, kc, :], x_dram[kc * 128:(kc + 1) * 128, :], writes=st.xb[kc])


def store_x(k, st, x_dram):
    for kc in range(KC):
        k.p.dma("sp", x_dram[kc * 128:(kc + 1) * 128, :], st.x[:, kc, :], reads=st.xb[kc], is_out=True)


def rmsnorm(k, st, gcol, keep=False, reuse=False):
    p = k.p
    for tg in range(NTG):
        ts = slice(tg * TG, (tg + 1) * TG)
        if reuse:
            rs, rsb = st.keep_rstd[tg]
        else:
            bank, bb = k.banks.next()
            for kc in range(KC):
                sq, sqb = st.sq.next()
                p.op("act", ("activation", dict(out=sq[:], in_=st.x[:, kc, ts], func=AF.Square)),
                     reads=[st.xb[kc][tg]], writes=[sqb])
                mm(k, bank[:], k.ones, sq[:], kc == 0, kc == KC - 1, [sqb] + k.cbufs, [bb], True)
            rs, rsb = st.keep_rstd[tg] if keep else st.rstd.next()
            p.op("act", ("activation", dict(out=rs[:], in_=bank[:], func=AF.Ln, bias=k.eps, scale=1.0 / D)),
                 reads=[bb] + k.cbufs, writes=[rsb])
            p.op("act", ("activation", dict(out=rs[:], in_=rs[:], func=AF.Exp, scale=-0.5)),
                 reads=[rsb], writes=[rsb])
        for kc in range(KC):
            p.op("dve", ("scalar_tensor_tensor", dict(
                out=st.h[:, kc, ts], in0=st.x[:, kc, ts], scalar=st.gains[:, gcol + kc:gcol + kc + 1], in1=rs[:],
                op0=ALU.mult, op1=ALU.mult)),
                 reads=[st.xb[kc][tg], rsb, st.gains_b], writes=[st.hb[tg]])


def linear_fm(k, st, w_dram, nload, cpl, nkc, rhs_fn, rhs_bufs_fn, evac, kpad=None, pre=()):
    kp = kpad or nkc
    pending = None
    for l in range(nload):
        if l < len(pre):
            slot, wb = pre[l]
        else:
            slot, wb = st.wload(w_dram[l], cpl * kp * 128)
        for ci in range(cpl):
            m = l * cpl + ci
            for tg in range(NTG):
                bank, bb = k.banks.next()
                for kc in range(nkc):
                    o = (ci * kp + kc) * 128
                    mm(k, bank[:], slot[:, o:o + 128], rhs_fn(kc, tg), kc == 0, kc == nkc - 1,
                       [wb] + rhs_bufs_fn(kc, tg), [bb], kc == nkc - 1)
                if pending is not None:
                    pending()
                    pending = None
                r_ = evac(m, tg, bank, bb)
                if callable(r_):
                    pending = r_
    if pending is not None:
        pending()


def evac_scaled(k, st, out_ap, bank, bb, scale, wbufs):
    st.evac_i += 1
    if st.evac_i % 2 == 0:
        k.p.op("act", ("activation", dict(out=out_ap, in_=bank[:], func=AF.Copy, scale=scale)),
               reads=[bb], writes=wbufs)
    else:
        k.p.op("dve", ("tensor_scalar", dict(out=out_ap, in0=bank[:], scalar1=scale, scalar2=None, op0=ALU.mult)),
               reads=[bb], writes=wbufs)


def evac_resid(k, st, m, tg, bank, bb):
    ts = slice(tg * TG, (tg + 1) * TG)
    k.p.op("dve", ("tensor_tensor", dict(out=st.x[:, m, ts], in0=st.x[:, m, ts], in1=bank[:], op=ALU.add)),
           reads=[bb], writes=[st.xb[m][tg]])


def linear_tm_v(k, st, w_dram, nload, ncol, out_fn):
    p = k.p
    for l in range(nload):
        slot, wb = st.wload(w_dram[l], KC * ncol)
        vt, vb = st.stg.next()
        vacc = vt.rearrange("p (t c) -> p t c", c=ncol)
        for tb in range(8):
            tg = tb // 4
            bank, bb = k.banks.next()
            for kc in range(KC):
                mm(k, bank[:, 0:ncol], st.h[:, kc, tb * 128:(tb + 1) * 128], slot[:, kc * ncol:(kc + 1) * ncol],
                   kc == 0, kc == KC - 1, [wb, st.hb[tg]], [bb], kc == KC - 1)
            st.evac_i += 1
            if st.evac_i % 2 == 0:
                p.op("act", ("activation", dict(
                    out=vacc[:, tb, :], in_=bank[:, 0:ncol], func=AF.Copy)), reads=[bb], writes=[vb])
            else:
                p.op("dve", ("tensor_copy", dict(
                    out=vacc[:, tb, :], in_=bank[:, 0:ncol])), reads=[bb], writes=[vb])
        out_fn(l, vacc, vb)


def ffn(k, st, wgu, wd, gcol):
    p = k.p
    rmsnorm(k, st, gcol)
    acts = [(st.qo[:, 0:8, :], Buf("act0")), (st.qo[:, 8:16, :], Buf("act1"))]
    for a in range(2):
        for m in range(8 * a, 8 * a + 8):
            for tg in range(NTG):
                b = st.qob[m][tg]
                if b.w is not None:
                    acts[a][1].r.append(b.w)
                acts[a][1].r.extend(b.r)
    f0 = 0
    for q in range(NQ):
        fq = FQS[q]
        act, ab = acts[q % 2]
        for f in range(fq):
            slot, wb = st.wload(wgu[f0 + f], 2 * KC * 128)
            for tg in range(NTG):
                ts = slice(tg * TG, (tg + 1) * TG)
                bg, bgb = k.banks.next()
                bu, bub = k.banks.next()
                for kc in range(KC):
                    mm(k, bg[:], slot[:, kc * 128:(kc + 1) * 128], st.h[:, kc, ts], kc == 0, kc == KC - 1,
                       [wb, st.hb[tg]], [bgb], kc == KC - 1)
                for kc in range(KC):
                    o = (KC + kc) * 128
                    mm(k, bu[:], slot[:, o:o + 128], st.h[:, kc, ts], kc == 0, kc == KC - 1,
                       [wb, st.hb[tg]], [bub], kc == KC - 1)
                sg, sgb = st.tmpf.next()
                p.op("act", ("activation", dict(out=sg[:], in_=bg[:], func=AF.Silu)),
                     reads=[bgb], writes=[sgb])
                p.op("dve", ("tensor_tensor", dict(
                    out=act[:, f, ts], in0=sg[:], in1=bu[:], op=ALU.mult)),
                     reads=[sgb, bub], writes=[ab])
        f0 += fq
        linear_fm(k, st, wd[q], 8, 2, fq,
                  lambda kc, tg, act=act: act[:, kc, tg * TG:(tg + 1) * TG],
                  lambda kc, tg, ab=ab: [ab],
                  lambda m, tg, bank, bb: evac_resid(k, st, m, tg, bank, bb), kpad=8)


def qknorm_rope_chunk(k, st, rp, bank, bb, tg, out_ap, out_bufs, extra_scale):
    p = k.p
    ts = slice(tg * TG, (tg + 1) * TG)
    raw, rawb = st.tmpf.next()
    p.op("act", ("activation", dict(out=raw[:], in_=bank[:], func=AF.Copy)), reads=[bb], writes=[rawb])
    sq, sqb = st.sq.next()
    p.op("act", ("activation", dict(out=sq[:], in_=raw[:], func=AF.Square)), reads=[rawb], writes=[sqb])
    t1, t1b = st.tmpf.next()
    p.op("dve", ("tensor_tensor", dict(out=t1[:], in0=raw[:], in1=rp.cg[:, ts], op=ALU.mult)),
         reads=[rawb, rp.tab_b], writes=[t1b])

    def cont():
        bs, bsb = k.banks.next()
        mm(k, bs[:], rp.perm[:], raw[:], True, True, [rawb, rp.perm_b], [bsb], True)
        b2, b2b = k.banks.next()
        mm(k, b2[:], k.ones, sq[:], True, True, [sqb] + k.cbufs, [b2b], True)
        rs, rsb = st.rstd.next()
        p.op("act", ("activation", dict(out=rs[:], in_=b2[:], func=AF.Ln, bias=k.eps, scale=1.0 / 128)),
             reads=[b2b] + k.cbufs, writes=[rsb])
        p.op("act", ("activation", dict(out=rs[:], in_=rs[:], func=AF.Exp, scale=-0.5)), reads=[rsb], writes=[rsb])
        sw, swb = rp.sw.next()
        p.op("dve", ("tensor_tensor", dict(out=sw[:], in0=bs[:], in1=rp.sg[:, ts], op=ALU.mult)),
             reads=[bsb, rp.tab_b], writes=[swb])
        p.op("dve", ("tensor_tensor", dict(out=t1[:], in0=t1[:], in1=sw[:], op=ALU.add)),
             reads=[t1b, swb], writes=[t1b])
        p.op("dve", ("scalar_tensor_tensor", dict(out=out_ap, in0=t1[:], scalar=float(extra_scale), in1=rs[:],
                                                     op0=ALU.mult, op1=ALU.mult)),
             reads=[t1b, rsb], writes=out_bufs)
    return cont


class RopeState:
    def __init__(self, k, mem):
        A3 = mem.A3
        self.sw = Ring([(A3[:, 4096 + i * 512:4096 + (i + 1) * 512], Buf("sw%d" % i)) for i in range(2)])
        self.cg = A3[:, 5120:6144]
        self.sg = A3[:, 6144:7168]
        self.tab_b = Buf("ropetab")
        self.gn = mem.gn
        self.gn_b = mem.gn_b
        self.perm = mem.perm
        self.perm_b = mem.perm_b


def rope_tables(k, rp, cos_d, sin_d, col):
    p = k.p
    p.dma("sp", rp.cg, cos_d, writes=[rp.tab_b])
    p.dma("sp", rp.sg, sin_d, writes=[rp.tab_b])
    p.op("dve", ("tensor_scalar", dict(out=rp.cg, in0=rp.cg, scalar1=rp.gn[:, col:col + 1], scalar2=None,
                                          op0=ALU.mult)), reads=[rp.gn_b, rp.tab_b], writes=[rp.tab_b])
    p.op("dve", ("tensor_scalar", dict(out=rp.sg, in0=rp.sg, scalar1=rp.gn[:, col + 1:col + 2],
                                          scalar2=None, op0=ALU.mult)), reads=[rp.gn_b, rp.tab_b], writes=[rp.tab_b])


def attn_a_units():
    units = []
    for hh in range(4):
        for g in range(8):
            jmax = 4 * g + 3
            for j in range(jmax, -1, -1):
                c0 = max(0, (j - 4 * g) * 128)
                units.append(dict(hh=hh, g=g, j=j, c0=c0, first=(j == jmax), last=(j == 0), diag=(j >= 4 * g)))
    return units


def attn_a(k, mem, q_sb, k_sb, v_sb, qb, kb, vb, on_head_done=None, on_group_done=None):
    p = k.p
    allb = k.banks.items
    P1 = Ring(allb[0:2])
    P2 = Ring(allb[2:4])
    PO = Ring(allb[4:6])
    PC = Ring(allb[6:8])
    att = mem.att
    eb = Ring([(att[:, i * 512:(i + 1) * 512], Buf("eb%d" % i)) for i in range(2)])
    spb = Ring([(att[:, 1024 + i * 256:1024 + (i + 1) * 256].bitcast(BF16), Buf("sp%d" % i)) for i in range(2)])
    ab = Ring([(att[:, 1536 + i * 256:1536 + (i + 1) * 256].bitcast(BF16), Buf("ab%d" % i)) for i in range(2)])
    cb = Ring([(att[0:33, 2048 + i * 256:2048 + (i + 1) * 256].bitcast(BF16), Buf("cb%d" % i)) for i in range(2)])
    cfr = Ring([(att[0:33, 2560:3072], Buf("cfr0"))])
    units = attn_a_units()
    n = len(units)
    C = k.cbufs

    def s1(u):
        hh, g, j, c0 = u["hh"], u["g"], u["j"], u["c0"]
        u["P1"] = P1.next()
        bank, bb = u["P1"]
        kT = k_sb[:, hh, j * 128:(j + 1) * 128]
        qc = q_sb[:, hh, g * TG + c0:(g + 1) * TG]
        mm(k, bank[:, c0:TG], kT, qc, True, not u["diag"], kb[hh] + [qb[hh][g]], [bb], not u["diag"])
        if u["diag"]:
            mm(k, bank[:, c0:c0 + 128], k.ident, k.maskA, False, True, C, [bb], True)

    def actA1(u):
        c0 = u["c0"]
        bank, bb = u["P1"]
        u["eb"] = eb.next()
        e_, ebb = u["eb"]
        p.op("act", ("activation", dict(out=e_[:, c0:TG], in_=bank[:, c0:TG], func=AF.Exp)),
             reads=[bb], writes=[ebb])

    def actA2(u):
        c0 = u["c0"]
        e_, ebb = u["eb"]
        u["sp"] = spb.next()
        sp, spbb = u["sp"]
        p.op("act", ("activation", dict(out=sp[:, c0:TG], in_=e_[:, c0:TG], func=AF.Ln, bias=k.one, scale=1.0)),
             reads=[ebb] + C, writes=[spbb])

    def s2(u, nxt):
        hh, g, j, c0 = u["hh"], u["g"], u["j"], u["c0"]
        sp, spbb = u["sp"]
        if u["first"]:
            u["PO"] = PO.next()
            mm(k, u["PO"][0][:], k.ident, k.zeros[:], True, False, C, [u["PO"][1]], True)
            u["cf"] = cfr.next()
            p.op("dve", ("memset", dict(_pos=(u["cf"][0][:, :], 0.0))), writes=[u["cf"][1]])
            u["cb"] = None
        pc, pcb = PC.next()
        mm(k, pc[:, c0:TG], k.negones[:], sp[:, c0:TG], True, True, [spbb] + C, [pcb], True)
        if not u["last"]:
            nc0 = nxt["c0"]
            cft, cfb = u["cf"]
            p.op("dve", ("tensor_tensor", dict(out=cft[:, c0:TG], in0=cft[:, c0:TG], in1=pc[0:33, c0:TG],
                                               op=ALU.add)), reads=[pcb, cfb], writes=[cfb])
            cbt, cbb = cb.next()
            p.op("dve", ("tensor_copy", dict(out=cbt[:, nc0:TG], in_=cft[:, nc0:TG])),
                 reads=[cfb], writes=[cbb])
            p.op("dve", ("tensor_tensor", dict(out=cbt[32:33, nc0:TG], in0=cft[32:33, nc0:TG],
                                               in1=cbt[32:33, nc0:TG], op=ALU.subtract)),
                 reads=[cfb, cbb], writes=[cbb])
            nxt["cb"] = (cbt, cbb)
            nxt["PO"] = u["PO"]
            nxt["cf"] = u["cf"]
        u["P2"] = P2.next()
        bank, bb = u["P2"]
        kT = k_sb[:, hh, j * 128:(j + 1) * 128]
        qc = q_sb[:, hh, g * TG + c0:(g + 1) * TG]
        mm(k, bank[:, c0:TG], kT, qc, True, False, kb[hh] + [qb[hh][g]], [bb], False)
        has_c = u["cb"] is not None
        lastmm = not (has_c or u["diag"])
        mm(k, bank[:, c0:TG], k.negtri, sp[:, c0:TG], False, lastmm, [spbb] + C, [bb], lastmm)
        if has_c:
            cbt, cbb = u["cb"]
            lastmm = not u["diag"]
            mm(k, bank[:, c0:TG], k.sel33, cbt[:, c0:TG], False, lastmm, [cbb] + C, [bb], lastmm)
        if u["diag"]:
            mm(k, bank[:, c0:c0 + 128], k.ident, k.maskA, False, True, C, [bb], True)

    def actB(u):
        c0 = u["c0"]
        bank, bb = u["P2"]
        u["ab"] = ab.next()
        a_, abb = u["ab"]
        p.op("act", ("activation", dict(out=a_[:, c0:TG], in_=bank[:, c0:TG], func=AF.Exp)),
             reads=[bb], writes=[abb])

    def s3(u):
        hh, g, j, c0 = u["hh"], u["g"], u["j"], u["c0"]
        a_, abb = u["ab"]
        po, pob = u["PO"]
        vj = v_sb[:, hh, j * 128:(j + 1) * 128]
        mm(k, po[:, c0:TG], vj, a_[:, c0:TG], False, u["last"], vb[hh] + [abb], [pob], True)
        if u["last"]:
            p.op("dve", ("tensor_copy", dict(out=q_sb[:, hh, g * TG:(g + 1) * TG], in_=po[:])),
                 reads=[pob], writes=[qb[hh][g]])
            if on_group_done is not None:
                on_group_done(hh, g)
            if g == 7 and on_head_done is not None:
                on_head_done(hh)

    s1(units[0])
    for i in range(n + 1):
        if i + 1 < n:
            s1(units[i + 1])
        if i < n:
            actA1(units[i])
            actA2(units[i])
            s2(units[i], units[i + 1] if not units[i]["last"] else None)
        if i >= 1:
            actB(units[i - 1])
            s3(units[i - 1])


def attn_b(k, mem, q_sb, k_sb, v_sb, qb, kb, vb, on_head_done=None, on_group_done=None):
    p = k.p
    C = k.cbufs
    lam, lam2, sl, onesf = mem.lam, mem.lam2, mem.sl, mem.onesf
    lb, slb = mem.lam_b, mem.sl_b
    p.op("dve", ("memset", dict(_pos=(onesf[:], 1.0))), writes=[lb])
    p.op("dve", ("tensor_tensor", dict(out=lam2[:, 0:1], in0=lam[:, 0:1], in1=lam[:, 1:2], op=ALU.mult)),
         reads=[lb], writes=[lb])
    p.op("dve", ("tensor_tensor", dict(out=lam2[:, 1:2], in0=lam[:, 2:3], in1=lam[:, 3:4], op=ALU.mult)),
         reads=[lb], writes=[lb])
    bank, bb = k.banks.next()
    mm(k, bank[:, 0:2], onesf[:], lam2[:, 0:2], True, True, [lb], [bb], True)
    p.op("act", ("activation", dict(out=lam2[:, 2:4], in_=bank[:, 0:2], func=AF.Exp)), reads=[bb], writes=[lb])
    p.op("dve", ("scalar_tensor_tensor", dict(out=lam[:, 0:1], in0=lam2[:, 3:4], scalar=-LAMBDA_INIT,
                                                 in1=lam2[:, 2:3], op0=ALU.add, op1=ALU.subtract)),
         reads=[lb], writes=[lb])
    neglam = lam[:, 0:1]
    p.op("dve", ("tensor_scalar", dict(out=sl[:], in0=sl[:], scalar1=1.0 - LAMBDA_INIT, scalar2=None,
                                          op0=ALU.mult)), reads=[slb], writes=[slb])

    allb = k.banks.items
    P1 = Ring(allb[0:3])
    POS = Ring([[allb[3], allb[4]], [allb[5], allb[6]]])
    PL = allb[7]
    PX = P1
    att = mem.att
    pb = Ring([(att[:, o_:o_ + 256].bitcast(BF16), Buf("pb%d" % i)) for i, o_ in enumerate((0, 256, 2816))])
    on = [(att[:, 512 + i * 1024:512 + (i + 1) * 1024].rearrange("p (a t) -> p a t", a=2), Buf("on%d" % i))
          for i in range(2)]
    sqr = Ring([(att[:, 2560:2816].bitcast(BF16), Buf("sqr0"))])
    rl = mem.rstd.items[0]
    rsd = mem.rstd.items[1]

    for hh in range(2):
        for g in range(8):
            for c in range(2):
                hc = 2 * hh + c
                units = []
                for j in range(4 * g + 4):
                    c0 = max(0, (j - 4 * g) * 128)
                    units.append((j, c0, j >= 4 * g))
                PO = POS.next()
                for (bk, bkb) in (PO[0], PO[1]):
                    mm(k, bk[:], k.ident, k.zeros[:], True, False, C, [bkb], True)
                pq = []
                nu = len(units)
                st_ = {"plz": False}

                def av(item, PO=PO, st_=st_):
                    j, c0, pt, ptb, last = item
                    if not st_["plz"]:
                        mm(k, PL[0][:], k.ident, k.zeros[:], True, False, C, [PL[1]], True)
                        st_["plz"] = True
                    for half in range(2):
                        vj = v_sb[:, hh, j * 256 + half * 128: j * 256 + half * 128 + 128]
                        mm(k, PO[half][0][:, c0:TG], vj, pt[:, c0:TG], False, last, vb[hh] + [ptb],
                           [PO[half][1]], True)
                    mm(k, PL[0][:, c0:TG], k.ones, pt[:, c0:TG], False, last, [ptb] + C, [PL[1]], True)

                for ui in range(nu):
                    j, c0, diag = units[ui]
                    bank, bb = P1.next()
                    mm(k, bank[:, c0:TG], k_sb[:, hc, j * 128:(j + 1) * 128],
                       q_sb[:, hc, g * TG + c0:(g + 1) * TG], True, not diag, kb[hc] + [qb[hc][g]], [bb], not diag)
                    if diag:
                        mm(k, bank[:, c0:c0 + 128], k.ident, k.maskB, False, True, C, [bb], True)
                    pt, ptb = pb.next()
                    p.op("act", ("activation", dict(
                        out=pt[:, c0:TG], in_=bank[:, c0:TG], func=AF.Exp)), reads=[bb], writes=[ptb])
                    pq.append((j, c0, pt, ptb, ui == nu - 1))
                    if len(pq) > 2:
                        av(pq.pop(0))
                while pq:
                    av(pq.pop(0))
                p.op("act", ("activation", dict(out=rl[0][:], in_=PL[0][:], func=AF.Ln)), reads=[PL[1]], writes=[rl[1]])
                p.op("act", ("activation", dict(out=rl[0][:], in_=rl[0][:], func=AF.Exp, scale=-1.0)),
                     reads=[rl[1]], writes=[rl[1]])
                for half in range(2):
                    p.op("dve", ("tensor_tensor", dict(
                        out=on[c][0][:, half, :], in0=PO[half][0][:], in1=rl[0][:], op=ALU.mult)),
                         reads=[PO[half][1], rl[1]], writes=[on[c][1]])
            o_, o_b = on[0]
            p.op("dve", ("scalar_tensor_tensor", dict(out=o_[:], in0=on[1][0][:], scalar=neglam, in1=o_[:],
                                                         op0=ALU.mult, op1=ALU.add)),
                 reads=[on[1][1], o_b, lb], writes=[o_b])
            bx, bxb = PX.next()
            for half in range(2):
                sq, sqb = sqr.next()
                p.op("act", ("activation", dict(out=sq[:], in_=o_[:, half, :],
                                                                     func=AF.Square)), reads=[o_b], writes=[sqb])
                mm(k, bx[:], k.ones, sq[:], half == 0, half == 1, [sqb] + C, [bxb], True)
            p.op("act", ("activation", dict(out=rsd[0][:], in_=bx[:], func=AF.Ln, bias=k.eps,
                                                      scale=1.0 / 256)), reads=[bxb] + C, writes=[rsd[1]])
            p.op("act", ("activation", dict(out=rsd[0][:], in_=rsd[0][:], func=AF.Exp, scale=-0.5)),
                 reads=[rsd[1]], writes=[rsd[1]])
            for half in range(2):
                p.op("dve", ("scalar_tensor_tensor", dict(
                    out=q_sb[:, 2 * hh + half, g * TG:(g + 1) * TG], in0=o_[:, half, :],
                    scalar=sl[:, half:half + 1], in1=rsd[0][:], op0=ALU.mult, op1=ALU.mult)),
                     reads=[o_b, rsd[1], slb], writes=[qb[2 * hh + half][g]])
            if on_group_done is not None:
                on_group_done(hh, g)
            if on_head_done is not None and g in (3, 7):
                on_head_done(hh, g // 4)


def build_fused():
    nc = bass.Bass("TRN2", target_bir_lowering=False)
    GROUPS = [[0, 1, 2, 3], [4, 5, 6, 7]]
    with ExitStack() as stack:
        k = K(nc, stack)
        p = k.p
        x_d = k.din("xT", [D, T])
        cm_d = k.din("cm", [128, 768], BF16)
        g_d = k.din("gains", [128, 8 * KC])
        gn_d = k.din("gn", [128, 4])
        idx_d = k.din("idx", [128, 4], mybir.dt.uint32)
        cos_d = k.din("cosT", [128, T])
        sin_d = k.din("sinT", [128, T])
        lam_d = k.din("lamv", [128, 4])
        sl_d = k.din("subln", [128, 2])
        perm_d = k.din("perm", [128, 128])
        wqk_d = k.din("wqk", [16, 128, 4096])
        wva_d = k.din("wv_a", [8, 128, 4096])
        woa1_d = k.din("wo_a1", [8, 128, 3072])
        woa2_d = k.din("wo_a2", [4, 128, 2048])
        wgua_d = k.din("wgu_a", [FC, 128, 4096])
        wda_d = k.din("wd_a", [NQ, 8, 128, 2048])
        wk_d = k.din("wk_s", [8, 128, 4096])
        wvs_d = k.din("wv_s", [8, 128, 4096])
        wq_d = k.din("wq_b", [8, 128, 4096])
        wob1_d = k.din("wo_b1", [8, 128, 2048])
        wob2_d = k.din("wo_b2", [8, 128, 2048])
        wgub_d = k.din("wgu_b", [FC, 128, 4096])
        wdb_d = k.din("wd_b", [NQ, 8, 128, 2048])
        x_o = k.dout("xo", [D, T])
        exA_in = nc.dram_tensor("exA_in", [12 * 512, 1024], BF16).ap()
        exA_out = nc.dram_tensor("exA_out", [12 * 2048, 1024], BF16).ap()
        exO_in = nc.dram_tensor("exO_in", [4 * 512, 1024], BF16).ap()
        exO_out = nc.dram_tensor("exO_out", [4 * 2048, 1024], BF16).ap()
        warm_in = nc.dram_tensor("warm_in", [16, 1024], BF16).ap()
        warm_out = nc.dram_tensor("warm_out", [64, 1024], BF16).ap()
        k.consts(cm_d)
        mem = Mem(k)
        mem.lam = k.sb([128, 4], F32, "lam")
        mem.lam2 = k.sb([128, 4], F32, "lam2")
        mem.sl = k.sb([128, 2], F32, "sl")
        mem.onesf = k.sb([128, 128], F32, "onesf")
        mem.perm = k.sb([128, 128], F32, "perm")
        mem.perm_b = Buf("perm")
        mem.lam_b = Buf("lam")
        mem.sl_b = Buf("sl")
        p.dma("sp", mem.gains[:], g_d, writes=[mem.gains_b])
        p.dma("sp", mem.gn[:], gn_d, writes=[mem.gn_b])
        p.dma("sp", mem.idx[:], idx_d, writes=[mem.idx_b])
        p.dma("sp", mem.lam[:], lam_d, writes=[mem.lam_b])
        p.dma("sp", mem.sl[:], sl_d, writes=[mem.sl_b])
        p.dma("sp", mem.perm[:], perm_d, writes=[mem.perm_b])
        idx0 = mem.idx[:, 0:1]
        idx1 = mem.idx[:, 1:2]
        idx2 = mem.idx[:, 2:3]

        def piece_ag(in_ap, out_ap, pc, dma_bufs, name):
            b = Buf(name)
            p.allgather(in_ap[pc * 512:(pc + 1) * 512, :], out_ap[pc * 2048:(pc + 1) * 2048, :], GROUPS,
                        reads=dma_bufs, writes=[b])
            return b

        def keep_warm(hh, g):
            if g != 7:
                p.allgather(warm_in, warm_out, GROUPS)

        cur = {}

        st = TokState(k, mem)
        load_x(k, st, x_d)
        rmsnorm(k, st, 0)
        hfn = lambda kc, tg: st.h[:, kc, tg * TG:(tg + 1) * TG]
        hbn = lambda kc, tg: [st.hb[tg]]
        ccA = {}

        late = []

        def sched(cc, pc, bufs, st_, name, early=True):
            fn = lambda: cc.__setitem__(pc, piece_ag(exA_in, exA_out, pc, bufs, name))
            if early:
                st_.defer(fn)
            else:
                late.append(fn)

        def run_late():
            for fn in late:
                fn()
            del late[:]

        for hh in range(4):
            pb_ = {0: [], 1: []}

            def ev(m, tg, bank, bb, hh=hh, pb_=pb_):
                t, j = m // 4, m % 4
                ts = slice(tg * TG, (tg + 1) * TG)
                if tg == 0:
                    cur["s"] = st.stg.next()
                s_, sb_ = cur["s"]
                evac_scaled(k, st, s_[:, ts], bank, bb, SCALE if t == 0 else 1.0, [sb_])
                if tg == 1:
                    pc = hh * 3 + t
                    r0 = pc * 512 + j * 128
                    db = Buf("dA")
                    p.dma("sp", exA_in[r0:r0 + 128, :], s_[:, 0:T], reads=[sb_], writes=[db])
                    pb_[t].append(db)
                    if j == 3:
                        sched(ccA, pc, list(pb_[t]), st, "ccA%d" % pc, early=(hh == 0))

            linear_fm(k, st, wqk_d[4 * hh:4 * hh + 4], 4, 2, KC, hfn, hbn, ev)
            vbufs = []

            def out_v_a(l, vacc, vb_, hh=hh, vbufs=vbufs):
                pc = hh * 3 + 2
                for e_ in range(2):
                    j = 2 * l + e_
                    r0 = pc * 512 + j * 128
                    db = Buf("dV")
                    p.dma("sp", exA_in[r0:r0 + 128, :].rearrange("p (t d) -> p t d", d=128),
                          vacc[:, :, e_ * 128:(e_ + 1) * 128], reads=[vb_], writes=[db])
                    vbufs.append(db)
                if l == 1:
                    sched(ccA, pc, list(vbufs), st, "ccA%d" % pc, early=(hh == 0))

            linear_tm_v(k, st, wva_d[2 * hh:2 * hh + 2], 2, 256, out_v_a)
        st.flush()
        p.barrier()

        q_sb = mem.v3(mem.A1, 4)
        k_sb = mem.v3(mem.A2, 4)
        v_sb = mem.A3[:, :].bitcast(BF16).rearrange("p (k t) -> p k t", k=4)
        qb = [[Buf("q%d_%d" % (h, g)) for g in range(8)] for h in range(4)]
        kb = [[Buf("k%d_%d" % (h, r)) for r in range(4)] for h in range(4)]
        vb = [[Buf("v%d_%d" % (h, r)) for r in range(4)] for h in range(4)]
        for hh in range(4):
            if hh == 1:
                run_late()
            for r in range(4):
                rs = slice(r * T, (r + 1) * T)
                for t, (dst, wb_) in enumerate(((q_sb, [qb[hh][2 * r], qb[hh][2 * r + 1]]), (k_sb, [kb[hh][r]]),
                                                (v_sb, [vb[hh][r]]))):
                    pc = hh * 3 + t
                    p.gather(dst[:, hh, rs], exA_out, idx0, (pc * 2048 + r * 512) * 1024,
                             reads=[ccA[pc], mem.idx_b], writes=wb_)
        def prefetch(w_dram, nl, n):
            out = []
            for l in range(nl):
                slot, b = mem.wslots.next()
                p.dma("pool", slot[:, 0:n], w_dram[l], writes=[b])
                out.append((slot, b))
            return out

        pre_wo = prefetch(woa1_d, 3, 3072)
        ccO = {}

        def o_done(hc):
            dbs = []
            for r in range(4):
                r0 = (hc * 4 + r) * 128
                db = Buf("dO")
                p.dma("sp", exO_in[r0:r0 + 128, :], q_sb[:, hc, r * T:(r + 1) * T],
                      reads=[qb[hc][2 * r], qb[hc][2 * r + 1]], writes=[db])
                dbs.append(db)
            ccO[hc] = piece_ag(exO_in, exO_out, hc, dbs, "ccO%d" % hc)

        attn_a(k, mem, q_sb, k_sb, v_sb, qb, kb, vb, on_head_done=o_done, on_group_done=keep_warm)
        p.barrier()

        def gather_o(st_, hcs):
            for hc in hcs:
                for j in range(4):
                    p.gather(st_.qo[:, hc * 4 + j, :], exO_out, idx0, (hc * 2048 + j * 512) * 1024,
                             reads=[ccO[hc], mem.idx_b], writes=st_.qob[hc * 4 + j])

        st = TokState(k, mem)
        gather_o(st, (0, 1, 2))
        hfn = lambda kc, tg: st.h[:, kc, tg * TG:(tg + 1) * TG]
        hbn = lambda kc, tg: [st.hb[tg]]
        qof = lambda kc, tg: st.qo[:, kc, tg * TG:(tg + 1) * TG]
        qobf = lambda kc, tg: [st.qob[kc][tg]]
        resid = lambda m, tg, bank, bb: evac_resid(k, st, m, tg, bank, bb)
        linear_fm(k, st, woa1_d, 8, 2, 12, qof, qobf, resid, pre=pre_wo)
        gather_o(st, (3,))
        linear_fm(k, st, woa2_d, 4, 4, 4, lambda kc, tg: st.qo[:, 12 + kc, tg * TG:(tg + 1) * TG],
                  lambda kc, tg: [st.qob[12 + kc][tg]], resid)
        ffn(k, st, wgua_d, wda_d, 16)
        rp = RopeState(k, mem)
        rope_tables(k, rp, cos_d, sin_d, 0)
        rmsnorm(k, st, 32, keep=True)
        ccB = {}

        def mk_ev_rope(hh, pc0, scale, st_):
            pb_ = {0: [], 1: []}

            def ev(m, tg, bank, bb):
                c, j = m // 4, m % 4
                ts = slice(tg * TG, (tg + 1) * TG)
                if tg == 0:
                    cur["s"] = st_.stg.next()
                s_, sb_ = cur["s"]
                cont = qknorm_rope_chunk(k, st_, rp, bank, bb, tg, s_[:, ts], [sb_], scale)

                def tail(cont=cont, s_=s_, sb_=sb_, tg=tg, c=c, j=j):
                    cont()
                    if tg == 1:
                        pc = hh * 6 + pc0 + c
                        r0 = pc * 512 + j * 128
                        db = Buf("dB")
                        p.dma("sp", exA_in[r0:r0 + 128, :], s_[:, 0:T], reads=[sb_], writes=[db])
                        pb_[c].append(db)
                        if j == 3:
                            sched(ccB, pc, list(pb_[c]), st_, "ccB%d" % pc, early=(hh == 0))
                return tail
            return ev

        for hh in range(2):
            linear_fm(k, st, wk_d[4 * hh:4 * hh + 4], 4, 2, KC, hfn, hbn, mk_ev_rope(hh, 2, 1.0, st))
            vbufs = []

            def out_v_s(l, vacc, vb_, hh=hh, vbufs=vbufs):
                pc = hh * 6 + 4 + l // 2
                r0 = pc * 512 + (l % 2) * 256
                dst = exA_in[r0:r0 + 256, :].rearrange("(p s) e -> p (s e)", s=2).rearrange("p (t c) -> p t c", c=256)
                db = Buf("dVs")
                p.dma("sp", dst, vacc, reads=[vb_], writes=[db])
                vbufs.append(db)
                if l % 2 == 1:
                    sched(ccB, pc, list(vbufs[-2:]), st, "ccB%d" % pc, early=(hh == 0))

            linear_tm_v(k, st, wvs_d[4 * hh:4 * hh + 4], 4, 256, out_v_s)
        rope_tables(k, rp, cos_d, sin_d, 2)
        rmsnorm(k, st, 48, reuse=True)
        for hh in range(2):
            linear_fm(k, st, wq_d[4 * hh:4 * hh + 4], 4, 2, KC, hfn, hbn, mk_ev_rope(hh, 0, SCALE, st))
        st.flush()
        p.barrier()

        v_sb2 = mem.A3[:, :].bitcast(BF16).rearrange("p (k t) -> p k t", k=2)
        qb = [[Buf("qB%d_%d" % (h, g)) for g in range(8)] for h in range(4)]
        kb = [[Buf("kB%d_%d" % (h, r)) for r in range(4)] for h in range(4)]
        vb = [[Buf("vB%d_%d" % (h, i)) for i in range(8)] for h in range(2)]
        for hh in range(2):
            if hh == 1:
                run_late()
            for c in range(2):
                hc = 2 * hh + c
                for r in range(4):
                    rs = slice(r * T, (r + 1) * T)
                    pcq, pck = hh * 6 + c, hh * 6 + 2 + c
                    p.gather(q_sb[:, hc, rs], exA_out, idx0, (pcq * 2048 + r * 512) * 1024,
                             reads=[ccB[pcq], mem.idx_b], writes=[qb[hc][2 * r], qb[hc][2 * r + 1]])
                    p.gather(k_sb[:, hc, rs], exA_out, idx0, (pck * 2048 + r * 512) * 1024,
                             reads=[ccB[pck], mem.idx_b], writes=[kb[hc][r]])
            for r in range(4):
                for s_ in range(2):
                    p.gather(v_sb2[:, hh, r * 2048 + s_ * 1024:r * 2048 + (s_ + 1) * 1024], exA_out, idx1,
                             ((hh * 6 + 4) * 2048 + r * 512 + s_) * 1024,
                             reads=[ccB[hh * 6 + 4], ccB[hh * 6 + 5], mem.idx_b], writes=[vb[hh][2 * r + s_]])
        pre_wo = prefetch(wob1_d, 3, 2048)
        ccO = {}

        def o_done_b(hh, a):
            pc = hh * 2 + a
            dbs = []
            for rr in range(2):
                r = 2 * a + rr
                for half in range(2):
                    hc = 2 * hh + half
                    r0 = pc * 512 + (rr * 2 + half) * 128
                    db = Buf("dOb")
                    p.dma("sp", exO_in[r0:r0 + 128, :], q_sb[:, hc, r * T:(r + 1) * T],
                          reads=[qb[hc][2 * r], qb[hc][2 * r + 1]], writes=[db])
                    dbs.append(db)
            ccO[pc] = piece_ag(exO_in, exO_out, pc, dbs, "ccOb%d" % pc)

        def gather_o_b(st_, hhs):
            for hh in hhs:
                for half in range(2):
                    for j in range(4):
                        sl_ = hh * 8 + half * 4 + j
                        p.gather(st_.qo[:, sl_, :], exO_out, idx2,
                                 (hh * 2 * 2048 + j * 512 + half * 128) * 1024,
                                 reads=[ccO[hh * 2], ccO[hh * 2 + 1], mem.idx_b], writes=st_.qob[sl_])

        attn_b(k, mem, q_sb, k_sb, v_sb2, qb, kb, vb, on_head_done=o_done_b, on_group_done=keep_warm)
        p.barrier()

        st = TokState(k, mem)
        gather_o_b(st, (0,))
        qof = lambda kc, tg: st.qo[:, kc, tg * TG:(tg + 1) * TG]
        qobf = lambda kc, tg: [st.qob[kc][tg]]
        resid = lambda m, tg, bank, bb: evac_resid(k, st, m, tg, bank, bb)
        linear_fm(k, st, wob1_d, 8, 2, 8, qof, qobf, resid, pre=pre_wo)
        gather_o_b(st, (1,))
        linear_fm(k, st, wob2_d, 8, 2, 8, lambda kc, tg: st.qo[:, 8 + kc, tg * TG:(tg + 1) * TG],
                  lambda kc, tg: [st.qob[8 + kc][tg]], resid)
        ffn(k, st, wgub_d, wdb_d, 64)
        store_x(k, st, x_o)
        p.emit()
    return nc


def w_fm(w, cpl):
    kd, n = w.shape
    kc = kd // 128
    a = w.reshape(kc, 128, n // 128, 128)
    a = a.transpose(2, 1, 0, 3)
    a = a.reshape(n // 128 // cpl, cpl, 128, kc, 128).transpose(0, 2, 1, 3, 4)
    return np.ascontiguousarray(a.reshape(n // 128 // cpl, 128, cpl * kc * 128))


def w_tm(w, ncol):
    kd, n = w.shape
    kc = kd // 128
    a = w.reshape(kc, 128, n // ncol, ncol).transpose(2, 1, 0, 3)
    return np.ascontiguousarray(a.reshape(n // ncol, 128, kc * ncol))


def w_gu(w):
    g = w_fm(w[:, :FF], 1)
    u = w_fm(w[:, FF:], 1)
    return np.ascontiguousarray(np.concatenate([g, u], axis=2))


def w_down(w):
    out = np.zeros((NQ, 8, 128, 2, 8, 128), np.float32)
    f0 = 0
    for q in range(NQ):
        fq = FQS[q]
        a = w_fm(w[f0 * 128:(f0 + fq) * 128, :], 2).reshape(8, 128, 2, fq, 128)
        out[q][:, :, :, :fq, :] = a
        f0 += fq
    return out.reshape(NQ, 8, 128, 2048)


def gain_fm(g):
    return np.ascontiguousarray(g.reshape(KC, 128).T)


def const_mats():
    i = np.arange(128)
    ident = np.eye(128, dtype=np.float32)
    negtri = -(i[:, None] >= i[None, :]).astype(np.float32)
    sel = np.zeros((128, 128), np.float32)
    sel[0, :] = 1.0
    sel[32, :] = 1.0
    maskA = np.where(i[:, None] >= i[None, :], NEG, 0.0).astype(np.float32)
    maskB = np.where(i[:, None] > i[None, :], NEG, 0.0).astype(np.float32)
    ones = np.ones((128, 128), np.float32)
    return np.concatenate([ident, negtri, sel, maskA, maskB, ones], axis=1).astype(BF)


def rope_tabs(pos):
    inv = ROPE_THETA ** (-np.arange(0, 128, 2, dtype=np.float32) / np.float32(128))
    ang = pos.astype(np.float32)[:, None] * inv.astype(np.float32)[None, :]
    ang = np.concatenate([ang, ang], axis=-1).astype(np.float32)
    cos = np.cos(ang).astype(np.float32).T
    sin = np.sin(ang).astype(np.float32).T.copy()
    sin[:64] *= -1.0
    return np.ascontiguousarray(cos), np.ascontiguousarray(sin)


_CACHE = {}


def kernel(x, a_attn_norm, a_w_qkv, a_w_o, a_ffn_norm, a_w_gate_up, a_w_down, kv_norm, w_kv, k_norm,
           b_attn_norm, b_w_q, b_q_norm, b_lambda_q1, b_lambda_k1, b_lambda_q2, b_lambda_k2,
           b_subln, b_w_o, b_ffn_norm, b_w_gate_up, b_w_down):
    f = lambda a: np.asarray(a, dtype=np.float32)
    x = f(x)
    gains = np.zeros((128, 8 * KC), np.float32)
    for i, g in enumerate([a_attn_norm[0], a_ffn_norm[0], kv_norm, b_attn_norm[0], b_ffn_norm[0]]):
        gains[:, i * KC:(i + 1) * KC] = gain_fm(f(g))
    kn = f(k_norm)
    qn = f(b_q_norm[0])
    gn = np.stack([kn, np.roll(kn, 64), qn, np.roll(qn, 64)], axis=1).astype(np.float32)
    lamv = np.ascontiguousarray(np.stack([f(v).reshape(128) for v in
                                          (b_lambda_q1[0], b_lambda_k1[0], b_lambda_q2[0], b_lambda_k2[0])], axis=1))
    subln = np.ascontiguousarray(f(b_subln[0]).reshape(2, 128).T)
    wqkv = f(a_w_qkv[0])
    wkv = f(w_kv)
    c128 = np.arange(128)
    cols_qk_a = np.concatenate([t * D + (4 * j + hh) * 128 + c128 for hh in range(4) for t in range(2) for j in range(4)])
    cols_v_a = np.concatenate([2 * D + (4 * j + hh) * 128 + c128 for hh in range(4) for j in range(4)])
    cols_k_s = np.concatenate([(4 * j + 2 * hh + c) * 128 + c128 for hh in range(2) for c in range(2) for j in range(4)])
    cols_v_s = np.concatenate([D + (2 * j + hh) * 256 + np.arange(256) for hh in range(2) for j in range(4)])
    def wo_split(w, slot_chunks, n1, cpl2):
        rows = np.concatenate([ch * 128 + c128 for ch in slot_chunks])
        wp = w[rows, :]
        return w_fm(wp[:n1 * 128], 2), w_fm(wp[n1 * 128:], cpl2)

    woa = wo_split(f(a_w_o[0]), [4 * (s_ % 4) + s_ // 4 for s_ in range(16)], 12, 4)
    wob = wo_split(f(b_w_o[0]), [4 * (s_ % 4) + 2 * (s_ // 8) + (s_ % 8) // 4 for s_ in range(16)], 8, 2)
    common = dict(
        cm=const_mats(), perm=np.ascontiguousarray(np.roll(np.eye(128, dtype=np.float32), 64, axis=0)), gains=gains, gn=gn, lamv=lamv, subln=subln,
        wqk=w_fm(wqkv[:, cols_qk_a], 2), wv_a=w_tm(wqkv[:, cols_v_a], 256),
        wo_a1=woa[0], wo_a2=woa[1], wgu_a=w_gu(f(a_w_gate_up[0])), wd_a=w_down(f(a_w_down[0])),
        wk_s=w_fm(wkv[:, cols_k_s], 2), wv_s=w_tm(wkv[:, cols_v_s], 256), wq_b=w_fm(f(b_w_q[0])[:, cols_k_s], 2),
        wo_b1=wob[0], wo_b2=wob[1], wgu_b=w_gu(f(b_w_gate_up[0])), wd_b=w_down(f(b_w_down[0])),
    )
    cores = [(b, r) for b in range(2) for r in range(4)]
    tabs = [rope_tabs(np.arange(r * T, (r + 1) * T)) for r in range(4)]
    pidx = np.arange(128, dtype=np.uint32)
    in_maps = []
    for (b, r) in cores:
        d = dict(common)
        d["xT"] = np.ascontiguousarray(x[b, r * T:(r + 1) * T, :].T)
        d["cosT"], d["sinT"] = tabs[r]
        d["idx"] = np.ascontiguousarray(np.stack([r * 128 + pidx, (r // 2) * 2048 + (r % 2) * 256 + 2 * pidx,
                                                  (r // 2) * 2048 + (r % 2) * 256 + pidx, 0 * pidx],
                                                 axis=1).astype(np.uint32))
        in_maps.append(d)
    if "nc" not in _CACHE:
        _CACHE["nc"] = build_fused()
    res = run_bass_kernel_spmd(_CACHE["nc"], in_maps, core_ids=list(range(8))).results
    out = np.empty((2, S, D), np.float32)
    for i, (b, r) in enumerate(cores):
        out[b, r * T:(r + 1) * T, :] = np.asarray(res[i]["xo"]).T
    return out
```
